# Optimizing a Trainium2 kernel written in Bass

```python
import functools
import jax, jax.numpy as jnp
from jax import lax
import numpy as np

D_MODEL = 1024
BATCH = 8
SEQ = 8192
DEPTH = 2
DEC_BATCH = 8
DEC_SEQ = 16
PAST_LEN = 2048

CHUNK = 64
N_META = 16
SB_HEADS = 8
SB_DIM = 64
SB_QBLOCK = 128
SB_W = SB_HEADS * SB_DIM
GLA_HEADS = 4
GLA_DK = D_MODEL // 2 // GLA_HEADS
GLA_DV = D_MODEL // GLA_HEADS
GLA_KW = GLA_HEADS * GLA_DK
GLA_VW = GLA_HEADS * GLA_DV
GLA_RANK = 16
GLA_TAU = 16.0
PEER_HEADS = 8
PEER_NKEYS = 128
PEER_NEXPERTS = PEER_NKEYS * PEER_NKEYS
PEER_DKEY = 256
PEER_HALF = PEER_DKEY // 2
PEER_TOPK = 16
PEER_BLOCK = 256
DN_ALPHA = float((2 * DEPTH) ** 0.25)
DN_BETA = float((8 * DEPTH) ** -0.25)
LN_EPS = 1e-5
RMS_EPS = 1e-6

IN_SPLITS = (SB_W, SB_W, SB_W, GLA_KW, GLA_KW, GLA_VW, GLA_VW, GLA_RANK, D_MODEL, D_MODEL)
IN_WIDTH = int(sum(IN_SPLITS))
SPLIT_POINTS = tuple(int(s) for s in np.cumsum(IN_SPLITS)[:-1])

kernel_name = 'hybrid_stickbreak_gla_peer_stream_step'

F32 = jnp.float32


def layer_norm(x, g, b):
    xf = x.astype(F32)
    mu = jnp.mean(xf, axis=-1, keepdims=True)
    var = jnp.mean(jnp.square(xf - mu), axis=-1, keepdims=True)
    return ((xf - mu) * lax.rsqrt(var + LN_EPS) * g + b).astype(x.dtype)


def post_norm(x, y, g, b):
    return layer_norm(DN_ALPHA * x + y, g, b)


def sb_block(q, k, v, q_pos, k_pos):
    z = jnp.einsum('bqhd,bkhd->bhqk', q.astype(F32), k.astype(F32)) * (SB_DIM ** -0.5)
    mask = k_pos[None, :] < q_pos[:, None]
    log_keep = jnp.where(mask, jax.nn.log_sigmoid(-z), 0.0)
    after = lax.cumsum(log_keep, axis=3, reverse=True) - log_keep
    w = jnp.where(mask, jnp.exp(jax.nn.log_sigmoid(z) + after), 0.0)
    return jnp.einsum('bhqk,bkhd->bqhd', w, v.astype(F32))


def stick_breaking(q, k, v, q_pos, k_pos):
    B, Tq = q.shape[0], q.shape[1]
    blk = min(SB_QBLOCK, Tq)
    nb = -(-Tq // blk)
    pad = nb * blk - Tq
    qp = jnp.pad(q, ((0, 0), (0, pad), (0, 0), (0, 0)))
    pp = jnp.concatenate([q_pos, q_pos[-1] + 1 + jnp.arange(pad, dtype=q_pos.dtype)])
    qb = qp.reshape(B, nb, blk, SB_HEADS, SB_DIM).transpose(1, 0, 2, 3, 4)
    pb = pp.reshape(nb, blk)
    out = lax.map(lambda a: sb_block(a[0], k, v, a[1], k_pos), (qb, pb))
    return out.transpose(1, 0, 2, 3, 4).reshape(B, nb * blk, SB_HEADS, SB_DIM)[:, :Tq]


def gla_scan(q, k, v, log_a, s0):
    B, L = q.shape[0], q.shape[1]
    nc = L // CHUNK

    def to_chunks(t):
        return t.astype(F32).reshape(B, nc, CHUNK, *t.shape[2:]).swapaxes(0, 1)

    causal = jnp.tril(jnp.ones((CHUNK, CHUNK), dtype=bool))[None, :, :, None, None]

    def step(S, inp):
        qc, kc, vc, gc = inp
        b = jnp.cumsum(gc, axis=1)
        inter = jnp.einsum('bthk,bhkv->bthv', qc * jnp.exp(b), S)
        diff = b[:, :, None] - b[:, None, :]
        decay = jnp.where(causal, jnp.exp(jnp.where(causal, diff, 0.0)), 0.0)
        scores = jnp.einsum('bthk,bshk,btshk->bhts', qc, kc, decay)
        intra = jnp.einsum('bhts,bshv->bthv', scores, vc)
        last = b[:, -1]
        S = jnp.exp(last)[..., None] * S + jnp.einsum(
            'bshk,bshv->bhkv', kc * jnp.exp(last[:, None] - b), vc)
        return S, inter + intra

    S, o = lax.scan(step, s0.astype(F32),
                    (to_chunks(q), to_chunks(k), to_chunks(v), to_chunks(log_a)))
    return o.swapaxes(0, 1).reshape(B, L, GLA_HEADS, GLA_DV), S


def mixer_project(h, w_in, w_gk2, b_gk):
    B, T = h.shape[0], h.shape[1]
    p = h @ w_in
    qa, ka, va, qb, kb, vb, rb, glr, ga, gb = jnp.split(p, SPLIT_POINTS, axis=-1)
    log_a = jax.nn.log_sigmoid((glr @ w_gk2 + b_gk).astype(F32)) / GLA_TAU
    sbs = (B, T, SB_HEADS, SB_DIM)
    return (qa.reshape(sbs), ka.reshape(sbs), va.reshape(sbs),
            qb.reshape(B, T, GLA_HEADS, GLA_DK) * (GLA_DK ** -0.5),
            kb.reshape(B, T, GLA_HEADS, GLA_DK),
            vb.reshape(B, T, GLA_HEADS, GLA_DV),
            log_a.reshape(B, T, GLA_HEADS, GLA_DK), rb, ga, gb)


def mixer_merge(h, sb_out, gla_out, rb, ga, gb, gla_norm_g, w_pa, w_pb, w_o):
    B, T = h.shape[0], h.shape[1]
    o = gla_out * lax.rsqrt(jnp.mean(jnp.square(gla_out), axis=-1, keepdims=True) + RMS_EPS) * gla_norm_g
    gla_y = (o.reshape(B, T, GLA_VW) * jax.nn.silu(rb.astype(F32))).astype(h.dtype) @ w_pb
    sb_y = sb_out.reshape(B, T, SB_W).astype(h.dtype) @ w_pa
    merged = jax.nn.sigmoid(ga) * sb_y + jax.nn.sigmoid(gb) * gla_y
    return merged @ w_o


def peer_block(xb, w_q, sub_keys, u, v):
    n = xb.shape[0]
    q = (xb @ w_q).astype(F32).reshape(n, PEER_HEADS, 2, PEER_HALF)
    s = jnp.einsum('nhpd,pkd->nhpk', q, sub_keys.astype(F32))
    sv, si = lax.top_k(s, PEER_TOPK)
    cand = (sv[:, :, 0, :, None] + sv[:, :, 1, None, :]).reshape(n, PEER_HEADS, PEER_TOPK * PEER_TOPK)
    cv, ci = lax.top_k(cand, PEER_TOPK)
    i1 = jnp.take_along_axis(si[:, :, 0], ci // PEER_TOPK, axis=-1)
    i2 = jnp.take_along_axis(si[:, :, 1], ci % PEER_TOPK, axis=-1)
    e = i1 * PEER_NKEYS + i2
    g = jax.nn.softmax(cv, axis=-1)
    ue = jnp.take(u, e, axis=0).astype(F32)
    a = jax.nn.gelu(jnp.einsum('nd,nhkd->nhk', xb.astype(F32), ue), approximate=False) * g
    ve = jnp.take(v, e, axis=0).astype(F32)
    return jnp.einsum('nhk,nhkd->nd', a, ve).astype(xb.dtype)


def peer(h, w_q, sub_keys, u, v):
    shp = h.shape
    xf = h.reshape(-1, D_MODEL)
    n = xf.shape[0]
    blk = min(PEER_BLOCK, n)
    nb = -(-n // blk)
    xp = jnp.pad(xf, ((0, nb * blk - n), (0, 0))).reshape(nb, blk, D_MODEL)
    out = lax.map(lambda xb: peer_block(xb, w_q, sub_keys, u, v), xp)
    return out.reshape(nb * blk, D_MODEL)[:n].reshape(shp)


def setup_inputs(seed: int = 0) -> dict:
    key = jax.random.key(seed)
    ks = jax.random.split(key, 26)

    def nrm(k, shape, scale):
        return jax.random.normal(k, shape, F32) * scale

    D = D_MODEL
    return {
        'x_prompt': nrm(ks[0], (BATCH, SEQ, D), 1.0),
        'x_sample': nrm(ks[1], (DEC_BATCH, DEC_SEQ, D), 1.0),
        'cache_k': nrm(ks[2], (DEPTH, DEC_BATCH, PAST_LEN, SB_HEADS, SB_DIM), 1.0),
        'cache_v': nrm(ks[3], (DEPTH, DEC_BATCH, PAST_LEN, SB_HEADS, SB_DIM), 1.0),
        'state_gla': nrm(ks[4], (DEPTH, DEC_BATCH, GLA_HEADS, GLA_DK, GLA_DV), 1.0),
        'meta': nrm(ks[5], (N_META, D), 1.0),
        'ln_in_g': 1.0 + nrm(ks[6], (D,), 0.02),
        'ln_in_b': nrm(ks[7], (D,), 0.02),
        'w_in': nrm(ks[8], (DEPTH, D, IN_WIDTH), D ** -0.5),
        'w_gk2': nrm(ks[9], (DEPTH, GLA_RANK, GLA_KW), GLA_RANK ** -0.5),
        'b_gk': nrm(ks[10], (DEPTH, GLA_KW), 0.1),
        'gla_norm_g': 1.0 + nrm(ks[11], (DEPTH, GLA_DV), 0.02),
        'w_pa': nrm(ks[12], (DEPTH, SB_W, D), SB_W ** -0.5),
        'w_pb': nrm(ks[13], (DEPTH, GLA_VW, D), GLA_VW ** -0.5),
        'w_o': nrm(ks[14], (DEPTH, D, D), DN_BETA * D ** -0.5),
        'ln1_g': 1.0 + nrm(ks[15], (DEPTH, D), 0.02),
        'ln1_b': nrm(ks[16], (DEPTH, D), 0.02),
        'peer_wq': nrm(ks[17], (DEPTH, D, PEER_HEADS * PEER_DKEY), D ** -0.5),
        'peer_subkeys': nrm(ks[18], (DEPTH, 2, PEER_NKEYS, PEER_HALF), PEER_HALF ** -0.5),
        'peer_u': nrm(ks[19], (DEPTH, PEER_NEXPERTS, D), D ** -0.5),
        'peer_v': nrm(ks[20], (DEPTH, PEER_NEXPERTS, D), DN_BETA * PEER_HEADS ** -0.5),
        'ln2_g': 1.0 + nrm(ks[21], (DEPTH, D), 0.02),
        'ln2_b': nrm(ks[22], (DEPTH, D), 0.02),
    }


def reference(x_prompt, x_sample, cache_k, cache_v, state_gla, meta, ln_in_g, ln_in_b,
              w_in, w_gk2, b_gk, gla_norm_g, w_pa, w_pb, w_o, ln1_g, ln1_b,
              peer_wq, peer_subkeys, peer_u, peer_v, ln2_g, ln2_b):
    B, S_len = x_prompt.shape[0], x_prompt.shape[1]
    Bd, T = x_sample.shape[0], x_sample.shape[1]
    L = N_META + S_len
    keep = min(S_len, PAST_LEN)

    meta_rows = jnp.broadcast_to(meta[None].astype(x_prompt.dtype), (B, N_META, D_MODEL))
    hp = layer_norm(jnp.concatenate([meta_rows, x_prompt], axis=1), ln_in_g, ln_in_b)
    hs = layer_norm(x_sample, ln_in_g, ln_in_b)

    p_pos = jnp.arange(L)
    s_kpos = jnp.arange(N_META + PAST_LEN + T)
    s_qpos = s_kpos[N_META + PAST_LEN:]
    front = (-N_META) % CHUNK
    back = (-(front + L)) % CHUNK
    s_back = (-T) % CHUNK

    def pad_p(t):
        return jnp.pad(t, ((0, 0), (front, back), (0, 0), (0, 0)))

    def pad_s(t):
        return jnp.pad(t, ((0, 0), (0, s_back), (0, 0), (0, 0)))

    nk_p, nv_p, st_p, nk_s, nv_s, st_s = [], [], [], [], [], []
    for l in range(DEPTH):
        qa, ka, va, qb, kb, vb, la, rb, ga, gb = mixer_project(hp, w_in[l], w_gk2[l], b_gk[l])
        sb = stick_breaking(qa, ka, va, p_pos, p_pos)
        s0 = jnp.zeros((B, GLA_HEADS, GLA_DK, GLA_DV), F32)
        go, sp = gla_scan(pad_p(qb), pad_p(kb), pad_p(vb), pad_p(la), s0)
        go = go[:, front:front + L]
        y = mixer_merge(hp, sb, go, rb, ga, gb, gla_norm_g[l], w_pa[l], w_pb[l], w_o[l])
        hp = post_norm(hp, y, ln1_g[l], ln1_b[l])
        hp = post_norm(hp, peer(hp, peer_wq[l], peer_subkeys[l], peer_u[l], peer_v[l]), ln2_g[l], ln2_b[l])
        nk_p.append(ka[:, L - keep:])
        nv_p.append(va[:, L - keep:])
        st_p.append(sp.astype(state_gla.dtype))
        meta_k = jnp.broadcast_to(ka[:1, :N_META], (Bd, N_META, SB_HEADS, SB_DIM))
        meta_v = jnp.broadcast_to(va[:1, :N_META], (Bd, N_META, SB_HEADS, SB_DIM))

        qa2, ka2, va2, qb2, kb2, vb2, la2, rb2, ga2, gb2 = mixer_project(hs, w_in[l], w_gk2[l], b_gk[l])
        k_all = jnp.concatenate([meta_k, cache_k[l].astype(ka2.dtype), ka2], axis=1)
        v_all = jnp.concatenate([meta_v, cache_v[l].astype(va2.dtype), va2], axis=1)
        sb2 = stick_breaking(qa2, k_all, v_all, s_qpos, s_kpos)
        go2, ss = gla_scan(pad_s(qb2), pad_s(kb2), pad_s(vb2), pad_s(la2), state_gla[l])
        go2 = go2[:, :T]
        y2 = mixer_merge(hs, sb2, go2, rb2, ga2, gb2, gla_norm_g[l], w_pa[l], w_pb[l], w_o[l])
        hs = post_norm(hs, y2, ln1_g[l], ln1_b[l])
        hs = post_norm(hs, peer(hs, peer_wq[l], peer_subkeys[l], peer_u[l], peer_v[l]), ln2_g[l], ln2_b[l])
        nk_s.append(ka2)
        nv_s.append(va2)
        st_s.append(ss.astype(state_gla.dtype))

    y_prompt = hp[:, N_META:]
    y_sample = hs
    new_k_prompt = jnp.stack(nk_p)
    new_v_prompt = jnp.stack(nv_p)
    state_prompt = jnp.stack(st_p)
    new_k_sample = jnp.stack(nk_s)
    new_v_sample = jnp.stack(nv_s)
    state_sample = jnp.stack(st_s)
    return (y_prompt, y_sample, new_k_prompt, new_v_prompt, state_prompt, new_k_sample, new_v_sample, state_sample)
```

```python
import numpy as np
from contextlib import ExitStack
import concourse.bass as bass
import concourse.mybir as mybir
from concourse.bass_utils import run_bass_kernel_spmd

F32 = mybir.dt.float32; BF16 = mybir.dt.bfloat16; I32 = mybir.dt.int32; U32 = mybir.dt.uint32
AF = mybir.ActivationFunctionType; ALU = mybir.AluOpType; AX = mybir.AxisListType

D = 1024; DEPTH = 2; NMETA = 16; TS = 16
INW = 6672
C_QA, C_KA, C_VA, C_QB, C_KB, C_VB, C_RB, C_GLR, C_GA, C_GB = 0, 512, 1024, 1536, 2048, 2560, 3584, 4608, 4624, 5648
ALPHA = float((2 * DEPTH) ** 0.25)
LN_EPS = 1e-5; RMS_EPS = 1e-6
NEXP = 16384
NDS = 24


class Bld:
    def __init__(self, nc, es):
        self.nc = nc
        self.E = {'pe': nc.tensor, 'act': nc.scalar, 'dve': nc.vector, 'pool': nc.gpsimd, 'sp': nc.sync}
        self.sem = {k: es.enter_context(nc.semaphore('s_' + k)) for k in self.E}
        self.cnt = {k: 0 for k in self.E}
        self.seen = {k: {} for k in self.E}
        self.dsem = [es.enter_context(nc.semaphore('d%d' % i)) for i in range(NDS)]
        self.dcnt = [0] * NDS
        self.dnext = 0
        self.dq = {}
        self.lastw = {}
        self.readers = {}
        self.stopped = False
        self.ps = [es.enter_context(nc.psum_tensor('ps%d' % i, [128, 512], F32)) for i in range(8)]

    @staticmethod
    def key(a):
        return a if isinstance(a, str) else (a.tensor.name if hasattr(a, "tensor") else a.name)

    def _wait(self, eng, toks):
        best = {}
        for t in toks:
            if t is None:
                continue
            sk, h, v, te = t
            if te == 'pe' and eng == 'pe':
                continue
            if self.seen[eng].get(sk, 0) >= v:
                continue
            if sk not in best or best[sk][2] < v:
                best[sk] = t
        for sk, (_, h, v, te) in best.items():
            self.E[eng].wait_ge(h, v)
            self.seen[eng][sk] = v

    def _sync(self, eng, rk, wk):
        toks = []
        for k in rk:
            toks.append(self.lastw.get(k))
            if k.startswith('ps'):
                toks.extend(t for t in self.readers.get(k, {}).values() if t[3] != eng)
        for k in wk:
            toks.append(self.lastw.get(k))
            toks.extend(self.readers.get(k, {}).values())
        self._wait(eng, toks)

    def _record(self, tok, rk, wk):
        for k in wk:
            self.lastw[k] = tok
            self.readers[k] = {}
        for k in rk:
            if k in wk:
                continue
            self.readers.setdefault(k, {})[tok[0]] = tok

    def op(self, eng, fn, reads, writes):
        if self.stopped:
            return None
        rk = {self.key(a) for a in reads if not isinstance(a, (int, float))}
        wk = {self.key(a) for a in writes}
        self._sync(eng, rk, wk)
        inst = fn()
        inst.then_inc(self.sem[eng], 1)
        self.cnt[eng] += 1
        self._record((eng, self.sem[eng], self.cnt[eng], eng), rk, wk)
        return inst

    def dma(self, q, out, in_, rkey=None, wkey=None, fn=None):
        if self.stopped:
            return None
        rk = {self.key(in_) if rkey is None else rkey}
        wk = {self.key(out) if wkey is None else wkey}
        rk.discard('-'); wk.discard('-')
        self._sync(q, rk, wk)
        inst = fn() if fn is not None else self.E[q].dma_start(out=out, in_=in_)
        i = self._dsel(q)
        self.dcnt[i] += 16
        inst.then_inc(self.dsem[i], 16)
        self._record((('d', i), self.dsem[i], self.dcnt[i], None), rk, wk)

    def _dsel(self, q):
        half = NDS // 2
        base = 0 if q == 'sp' else half
        j = self.dq.get(q, 0)
        self.dq[q] = (j + 1) % half
        return base + j

    def gather(self, out, table, idx, extra_reads=()):
        if self.stopped:
            return None
        rk = {self.key(idx)} | {self.key(a) for a in extra_reads}
        wk = {self.key(out)}
        self._sync('pool', rk, wk)
        inst = self.nc.gpsimd.indirect_dma_start(out=out, out_offset=None, in_=table,
                                                 in_offset=bass.IndirectOffsetOnAxis(ap=idx, axis=0))
        i = self._dsel('pool')
        self.dcnt[i] += 16
        inst.then_inc(self.dsem[i], 16)
        self._record((('d', i), self.dsem[i], self.dcnt[i], None), rk, wk)

    def barrier(self):
        if self.stopped:
            return None
        toks = [(k, self.sem[k], self.cnt[k], None) for k in self.E if self.cnt[k] > 0]
        toks += [(('d', i), self.dsem[i], self.dcnt[i], None) for i in range(NDS) if self.dcnt[i] > 0]
        for e in self.E:
            self._wait(e, [t for t in toks if t[0] != e])
        self.lastw = {}
        self.readers = {}

    def mm(self, out, lhsT, rhs, start=True, stop=True):
        rd = [lhsT, rhs] + ([] if start else [out])
        return self.op('pe', lambda: self.nc.tensor.matmul(out, lhsT=lhsT, rhs=rhs, start=start, stop=stop), rd, [out])

    def act(self, out, in_, func, bias=None, scale=None, accum_out=None):
        kw = {}
        rd = [in_]
        if bias is not None:
            kw['bias'] = bias
            rd.append(bias)
        if scale is not None:
            kw['scale'] = scale
            rd.append(scale)
        return self.op('act', lambda: self.nc.scalar.activation(out=out, in_=in_, func=func, **kw), rd, [out])

    def tt(self, eng, out, in0, in1, op):
        return self.op(eng, lambda: self.E[eng].tensor_tensor(out=out, in0=in0, in1=in1, op=op), [in0, in1], [out])

    def ts(self, eng, out, in0, s1, s2, op0, op1=None):
        if op1 is None:
            return self.op(eng, lambda: self.E[eng].tensor_scalar(out=out, in0=in0, scalar1=s1, scalar2=None, op0=op0),
                           [in0, s1], [out])
        return self.op(eng, lambda: self.E[eng].tensor_scalar(out=out, in0=in0, scalar1=s1, scalar2=s2, op0=op0, op1=op1),
                       [in0, s1, s2], [out])

    def stt(self, eng, out, in0, scalar, in1, op0, op1):
        return self.op(eng, lambda: self.E[eng].scalar_tensor_tensor(out=out, in0=in0, scalar=scalar, in1=in1, op0=op0, op1=op1),
                       [in0, scalar, in1], [out])

    def copy(self, eng, out, in_):
        if eng == 'act':
            return self.act(out, in_, AF.Copy)
        return self.op(eng, lambda: self.E[eng].tensor_copy(out=out, in_=in_), [in_], [out])

    def memset(self, eng, ap, val):
        return self.op(eng, lambda: self.E[eng].memset(ap, val), [], [ap])

    def red(self, eng, out, in_, op, axis=AX.X):
        return self.op(eng, lambda: self.E[eng].tensor_reduce(out=out, in_=in_, axis=axis, op=op), [in_], [out])


class _Stop(Exception):
    pass


def build(S, P, dbg=False, nlayers=DEPTH, stop_at=None):
    NT = S // 128
    TOK = S + 32
    KEEP = min(S, P)
    nc = bass.Bass("TRN2", target_bir_lowering=False)

    def din(name, shape, dt=F32):
        return nc.dram_tensor(name, shape, dt, kind="ExternalInput").ap()

    def dout(name, shape, dt=F32):
        return nc.dram_tensor(name, shape, dt, kind="ExternalOutput").ap()

    def dscr(name, shape, dt=F32):
        return nc.dram_tensor(name, shape, dt, kind=("ExternalOutput" if dbg else "Internal")).ap()

    xp = din("xp", [S, D]); xs = din("xs", [TS, D]); meta = din("meta", [NMETA, D])
    ckT = din("ckT", [DEPTH, 128, 4, P]); cvv = din("cv", [DEPTH, P, 512]); sg = din("sg", [DEPTH, 4, 128, 256])
    ln_in_g = din("ln_in_g", [1, D]); ln_in_b = din("ln_in_b", [1, D])
    w_in = din("w_in", [DEPTH, D, INW]); w_gk2 = din("w_gk2", [DEPTH, 16, 512]); b_gk = din("b_gk", [DEPTH, 512])
    gng = din("gng", [DEPTH, 256]); w_pa = din("w_pa", [DEPTH, 512, D]); w_pb = din("w_pb", [DEPTH, D, D])
    w_o = din("w_o", [DEPTH, D, D]); ln1_g = din("ln1_g", [DEPTH, D]); ln1_b = din("ln1_b", [DEPTH, D])
    wq = din("wq", [DEPTH, D, 2048]); skT = din("skT", [DEPTH, 2, 128, 128])
    puT = din("puT", [DEPTH, D, NEXP]); pv = din("pv", [DEPTH, NEXP, D])
    ln2_g = din("ln2_g", [DEPTH, D]); ln2_b = din("ln2_b", [DEPTH, D])

    yp = dout("yp", [S, D]); ys = dout("ys", [TS, D])
    nkp = dout("nkp", [DEPTH, KEEP, 512]); nvp = dout("nvp", [DEPTH, KEEP, 512]); stp = dout("stp", [DEPTH, 4, 128, 256])
    nks = dout("nks", [DEPTH, TS, 512]); nvs = dout("nvs", [DEPTH, TS, 512]); sts = dout("sts", [DEPTH, 4, 128, 256])

    Hs = [dscr("H0", [TOK, D]), dscr("H1", [TOK, D])]
    UT = dscr("UT", [DEPTH, D, NEXP], BF16); VB = dscr("VB", [DEPTH, NEXP, D], BF16); SEL = dscr("SEL", [TOK, 3, 128])
    QT = dscr("QT", [128, 4, TOK], BF16); KT = dscr("KT", [128, 4, TOK], BF16); Vs = dscr("Vs", [TOK, 512], BF16)
    GBY = dscr("GBY", [TOK, D]); SGA = dscr("SGA", [TOK, D]); X1 = dscr("X1", [TOK, D])

    tiles = [(16, 0, 'meta', 0)] + [(128, 16 + 128 * j, 'prompt', 128 * j) for j in range(NT)] + [(16, 16 + S, 'sample', 0)]

    with ExitStack() as es:
        b = Bld(nc, es)
        PS = b.ps
        pool = nc.gpsimd
        vec = nc.vector

        cur_ti = [0]

        def chk(name):
            if stop_at == name or stop_at == '%s@%d' % (name, cur_ti[0]):
                b.stopped = True

        uid = [0]

        def sb(name, shape, dt=F32, stack=es):
            uid[0] += 1
            return stack.enter_context(nc.sbuf_tensor("%s_%d" % (name, uid[0]), shape, dt))

        def aff(out, in_, pattern, cmp, base, cm, fill=0.0):
            b.op('pool', lambda: pool.affine_select(out=out, in_=in_, pattern=pattern, compare_op=cmp, fill=fill,
                                                    base=base, channel_multiplier=cm), [in_], [out])

        identF = sb("identF", [128, 128]); identB = sb("identB", [128, 128], BF16)
        maskLE = sb("maskLE", [128, 128]); triInc = sb("triInc", [128, 128]); triGT = sb("triGT", [128, 128])
        negT = sb("negT", [128, 128], BF16); negOnes = sb("negOnes", [128, 128], BF16)
        iota16 = sb("iota16", [128, 16]); onesrow = sb("onesrow", [1, 128])
        lng = sb("lng", [128, D]); lnb = sb("lnb", [128, D])
        ln_s1 = sb("ln_s1", [128, 1]); ln_s2 = sb("ln_s2", [128, 1]); ln_xc = sb("ln_xc", [128, D]); ln_j = sb("ln_j", [128, D])
        b.memset('pool', identF[:], 1.0)
        aff(identF[:], identF[:], [[-1, 128]], ALU.is_equal, 0, 1)
        b.copy('dve', identB[:], identF[:])
        b.memset('pool', maskLE[:], 1.0)
        aff(maskLE[:], maskLE[:], [[1, 128]], ALU.is_ge, 0, -1)
        b.ts('dve', triInc[:], maskLE[:], -1.0 / 16.0, None, ALU.mult)
        b.memset('pool', triGT[:], -1.0 / 16.0)
        aff(triGT[:], triGT[:], [[-1, 128]], ALU.is_gt, 0, 1)
        b.memset('pool', ln_j[:, 0:128], -1.0)
        aff(ln_j[:, 0:128], ln_j[:, 0:128], [[-1, 128]], ALU.is_ge, 0, 1)
        b.copy('dve', negT[:], ln_j[:, 0:128])
        b.memset('pool', negOnes[:], -1.0)
        b.op('pool', lambda: pool.iota(iota16[:], pattern=[[1, 16]], base=0, channel_multiplier=0,
                                       allow_small_or_imprecise_dtypes=True), [], [iota16])
        b.memset('pool', onesrow[:], 1.0)

        def load_ln(g_ap, b_ap):
            b.dma('sp', lng[:], g_ap.partition_broadcast(128))
            b.dma('sp', lnb[:], b_ap.partition_broadcast(128))

        def rsqrt_inplace(t):
            b.act(t, t, AF.Sqrt)
            b.op('dve', lambda: vec.reciprocal(out=t, in_=t), [t], [t])

        def layernorm(out, x, n):
            b.red('dve', ln_s1[:n], x, ALU.add)
            b.ts('dve', ln_s1[:n], ln_s1[:n], -1.0 / D, None, ALU.mult)
            b.ts('dve', ln_xc[:n], x, ln_s1[:n, 0:1], None, ALU.add)
            b.memset('dve', ln_s2[:n], 0.0)
            b.op('dve', lambda: vec.scalar_tensor_tensor(out=ln_j[:n], in0=ln_xc[:n], scalar=1.0, in1=ln_xc[:n],
                                                         op0=ALU.mult, op1=ALU.mult, accum_out=ln_s2[:n]),
                 [ln_xc], [ln_j, ln_s2])
            b.ts('dve', ln_s2[:n], ln_s2[:n], 1.0 / D, LN_EPS, ALU.mult, ALU.add)
            rsqrt_inplace(ln_s2[:n])
            b.stt('dve', ln_j[:n], ln_xc[:n], ln_s2[:n, 0:1], lng[:n], ALU.mult, ALU.mult)
            b.tt('dve', out, ln_j[:n], lnb[:n], ALU.add)

        def transpose_to(dstT, src, n, bf_src, banks, eng0='dve', eng1='act'):
            ident = identB if bf_src else identF
            for half in range(2):
                bank = banks[half]
                for c in range(4):
                    k = half * 4 + c
                    b.mm(bank[:, c * n:(c + 1) * n], src[:, k * 128:(k + 1) * 128], ident[:n, :n])
                b.copy(eng0 if half == 0 else eng1, dstT[:, half * 4:(half + 1) * 4, :n],
                       bank[:, 0:4 * n].rearrange("p (c t) -> p c t", c=4))

        engs3 = ['dve', 'act', 'dve']

        def load_cast(dst, src, rows, cols, stg, cnt):
            st = stg[cnt % 2]
            b.dma('sp', st[:rows, :cols], src)
            b.copy(engs3[cnt % 3], dst, st[:rows, :cols])

        iota128 = sb("iota128", [128, 128])
        b.op('pool', lambda: pool.iota(iota128[:], pattern=[[1, 128]], base=0, channel_multiplier=0,
                                       allow_small_or_imprecise_dtypes=True), [], [iota128])
        with ExitStack() as pp:
            pst = [sb("ppst%d" % i, [128, 2048], F32, pp) for i in range(3)]
            pob = [sb("ppob%d" % i, [128, 2048], BF16, pp) for i in range(3)]
            cnt = 0
            for l_ in range(nlayers):
                for k in range(8):
                    for c0 in range(0, NEXP, 2048):
                        st = pst[cnt % 3]; ob = pob[cnt % 3]
                        b.dma('sp', st[:], puT[l_, k * 128:(k + 1) * 128, c0:c0 + 2048], rkey='-')
                        b.copy('dve' if cnt % 2 == 0 else 'act', ob[:], st[:])
                        b.dma('pool', UT[l_, k * 128:(k + 1) * 128, c0:c0 + 2048], ob[:], wkey='-')
                        cnt += 1
                for r0_ in range(0, NEXP, 256):
                    st = pst[cnt % 3]; ob = pob[cnt % 3]
                    b.dma('sp', st[:].rearrange("p (j f) -> p j f", j=2),
                          pv[l_, r0_:r0_ + 256, :].rearrange("(j p) f -> p j f", p=128), rkey='-')
                    b.copy('dve' if cnt % 2 == 0 else 'act', ob[:], st[:])
                    b.dma('pool', VB[l_, r0_:r0_ + 256, :].rearrange("(j p) f -> p j f", p=128),
                          ob[:].rearrange("p (j f) -> p j f", j=2), wkey='-')
                    cnt += 1
            b.barrier()

        try:
            for l in range(nlayers):
                Hin = Hs[l % 2]; Hout = Hs[(l + 1) % 2]
                last = (l == DEPTH - 1)
                with ExitStack() as p1:
                    winb = sb("winb", [128, 8, INW], BF16, p1)
                    wpbb = sb("wpbb", [128, 8, D], BF16, p1)
                    wgk2 = sb("wgk2", [16, 512], F32, p1); bgk = sb("bgk", [1, 512], F32, p1)
                    gngb = sb("gngb", [128, 256], F32, p1)
                    with ExitStack() as p1s:
                        stg = [sb("stg%d" % i, [128, 2048], F32, p1s) for i in range(2)]
                        cnt = 0
                        for k in range(8):
                            for c0 in range(0, INW, 2048):
                                cw = min(2048, INW - c0)
                                load_cast(winb[:, k, c0:c0 + cw], w_in[l, k * 128:(k + 1) * 128, c0:c0 + cw], 128, cw, stg, cnt); cnt += 1
                            load_cast(wpbb[:, k, :], w_pb[l, k * 128:(k + 1) * 128, :], 128, D, stg, cnt); cnt += 1
                        b.barrier()
                    b.dma('sp', wgk2[:], w_gk2[l]); b.dma('sp', bgk[:], b_gk[l:l + 1, :])
                    b.dma('sp', gngb[:], gng[l].partition_broadcast(128))
                    if l == 0:
                        load_ln(ln_in_g[0], ln_in_b[0])
                    chk('p1w')

                    xt = [sb("xt%d" % i, [128, D], F32, p1) for i in range(2)]
                    hT_ = sb("hT", [128, 8, 128], BF16, p1)
                    glrT = sb("glrT", [16, 128], F32, p1)
                    ex = sb("ex", [128, 512], F32, p1); sp = sb("sp", [128, 512], F32, p1)
                    EbT = sb("EbT", [128, 512], F32, p1); EnbT = sb("EnbT", [128, 512], F32, p1); Erev = sb("Erev", [128, 512], F32, p1)
                    QeT = sb("QeT", [128, 4, 128], BF16, p1); KeT = sb("KeT", [128, 4, 128], BF16, p1)
                    Kd = sb("Kd", [128, 512], BF16, p1); vB = sb("vB", [128, D], BF16, p1)
                    Vb = sb("Vb", [128, 512], BF16, p1)
                    Vf = ex; Kf = sp
                    srb = sb("srb", [128, D], F32, p1); sgb = sb("sgb", [128, D], F32, p1)
                    osq = ln_xc; sga = ln_xc; og = ln_j; gby = ln_j
                    QTt = sb("QTt", [128, 4, 128], BF16, p1); KTt = sb("KTt", [128, 4, 128], BF16, p1)
                    scTm = sb("scTm", [128, 4, 128], BF16, p1)
                    Sst = {'p': sb("Sp", [128, 4, 256], F32, p1), 's': sb("Ss", [128, 4, 256], F32, p1)}
                    Sbf = {'p': sb("Spb", [128, 4, 256], BF16, p1), 's': sb("Ssb", [128, 4, 256], BF16, p1)}
                    ssq = sb("ssq", [128, 4], F32, p1)
                    ogb = sb("ogb", [128, D], BF16, p1)
                    ogT = sb("ogT", [128, 8, 128], BF16, p1)

                    b.memset('pool', Sst['p'][:], 0.0); b.memset('pool', Sbf['p'][:], 0.0)
                    b.dma('sp', Sst['s'][:], sg[l].rearrange("h k v -> k h v"))
                    b.copy('pool', Sbf['s'][:], Sst['s'][:])

                    def src_ap(ti):
                        n, t0, kind, r0 = tiles[ti]
                        if l > 0:
                            return Hin[t0:t0 + n, :]
                        return {'meta': meta, 'prompt': xp, 'sample': xs}[kind][r0:r0 + n, :]

                    b.dma('sp', xt[0][:tiles[0][0]], src_ap(0), rkey='-')
                    for ti, (n, t0, kind, r0) in enumerate(tiles):
                        x_ = xt[ti % 2]
                        cur_ti[0] = ti
                        if ti + 1 < len(tiles):
                            b.dma('sp', xt[(ti + 1) % 2][:tiles[ti + 1][0]], src_ap(ti + 1), rkey='-')
                        if l == 0:
                            h_ = srb
                            layernorm(h_[:n], x_[:n], n)
                            b.dma('pool', Hin[t0:t0 + n, :], h_[:n], wkey='-')
                            hsrc = h_
                        else:
                            hsrc = x_
                        transpose_to(hT_, hsrc[:n], n, False, (PS[0], PS[1]))
                        chk('p1t')
                        sk = 'p' if kind != 'sample' else 's'
                        S_ = Sst[sk]; Sb_ = Sbf[sk]

                        def fm_group(bank, col0, nchunk, width, boff=0):
                            for c in range(nchunk):
                                for k in range(8):
                                    b.mm(bank[:width, boff + c * n: boff + (c + 1) * n],
                                         winb[:, k, col0 + c * width: col0 + (c + 1) * width], hT_[:, k, :n],
                                         start=(k == 0), stop=(k == 7))

                        def tm_group(bank, col0, w=512):
                            for k in range(8):
                                b.mm(bank[:n, :w], hT_[:, k, :n], winb[:, k, col0:col0 + w], start=(k == 0), stop=(k == 7))

                        def v4(ap):
                            return ap.rearrange("p (c t) -> p c t", c=4)

                        fm_group(PS[2], C_GLR, 1, 16)
                        b.copy('dve', glrT[:, :n], PS[2][:16, :n])
                        b.mm(PS[3][:n, :], glrT[:16, :n], wgk2[:], start=True, stop=False)
                        b.mm(PS[3][:n, :], onesrow[0:1, :n], bgk[:], start=False, stop=True)
                        b.act(ex[:n], PS[3][:n, :], AF.Exp, scale=-1.0)
                        b.act(sp[:n], ex[:n], AF.Ln, bias=1.0)
                        for hh in range(4):
                            b.mm(PS[4][:, hh * n:(hh + 1) * n], sp[:n, hh * 128:(hh + 1) * 128], triInc[:n, :n])
                        b.mm(PS[5][:n, :], triGT[:n, :n], sp[:n, :])
                        b.act(EbT[:, :4 * n], PS[4][:, :4 * n], AF.Exp)
                        b.act(EnbT[:, :4 * n], PS[4][:, :4 * n], AF.Exp, scale=-1.0)
                        b.act(Erev[:n], PS[5][:n, :], AF.Exp)
                        fm_group(PS[6], C_QB, 4, 128)
                        b.stt('dve', QeT[:, :, :n], v4(PS[6][:, :4 * n]), 128.0 ** -0.5, v4(EbT[:, :4 * n]), ALU.mult, ALU.mult)
                        fm_group(PS[7], C_KB, 4, 128)
                        b.tt('dve', KeT[:, :, :n], v4(PS[7][:, :4 * n]), v4(EnbT[:, :4 * n]), ALU.mult)
                        tm_group(PS[2], C_KB)
                        b.tt('dve', Kd[:n], PS[2][:n, :], Erev[:n], ALU.mult)
                        tm_group(PS[3], C_VB); b.copy('act', vB[:n, 0:512], PS[3][:n, :])
                        tm_group(PS[4], C_VB + 512); b.copy('act', vB[:n, 512:1024], PS[4][:n, :])
                        chk('p1gate')
                        for hh in range(4):
                            b.mm(PS[5][:n, hh * n:(hh + 1) * n], KeT[:, hh, :n], QeT[:, hh, :n])
                        for hh in range(4):
                            b.tt('dve', scTm[:n, hh, :n], PS[5][:n, hh * n:(hh + 1) * n], maskLE[:n, :n], ALU.mult)
                        for hh in range(4):
                            ob = PS[6 + hh // 2][:n, (hh % 2) * 256:(hh % 2 + 1) * 256]
                            b.mm(ob, QeT[:, hh, :n], Sb_[:, hh, :], start=True, stop=False)
                            b.mm(ob, scTm[:n, hh, :n], vB[:n, hh * 256:(hh + 1) * 256], start=False, stop=True)
                        for hh in range(4):
                            db = PS[hh // 2][:, (hh % 2) * 256:(hh % 2 + 1) * 256]
                            b.mm(db, Kd[:n, hh * 128:(hh + 1) * 128], vB[:n, hh * 256:(hh + 1) * 256])
                        for hh in range(4):
                            db = PS[hh // 2][:, (hh % 2) * 256:(hh % 2 + 1) * 256]
                            b.stt('dve', S_[:, hh, :], S_[:, hh, :], EbT[:, hh * n + n - 1: hh * n + n], db, ALU.mult, ALU.add)
                        b.copy('act', Sb_[:], S_[:])
                        chk('p1gla')
                        for half in range(2):
                            b.act(osq[:n, half * 512:(half + 1) * 512], PS[6 + half][:n, :], AF.Square)
                        b.red('dve', ssq[:n], osq[:n].rearrange("p (h v) -> p h v", h=4), ALU.add)
                        b.ts('dve', ssq[:n], ssq[:n], 1.0 / 256.0, RMS_EPS, ALU.mult, ALU.add)
                        rsqrt_inplace(ssq[:n])
                        for hh in range(4):
                            b.stt('dve', og[:n, hh * 256:(hh + 1) * 256], PS[6 + hh // 2][:n, (hh % 2) * 256:(hh % 2 + 1) * 256],
                                  ssq[:n, hh:hh + 1], gngb[:n, :], ALU.mult, ALU.mult)
                        tm_group(PS[2], C_RB); b.act(srb[:n, 0:512], PS[2][:n, :], AF.Silu)
                        tm_group(PS[3], C_RB + 512); b.act(srb[:n, 512:1024], PS[3][:n, :], AF.Silu)
                        b.tt('dve', ogb[:n], og[:n], srb[:n], ALU.mult)
                        transpose_to(ogT, ogb[:n], n, True, (PS[4], PS[5]))
                        tm_group(PS[6], C_GB); b.act(sgb[:n, 0:512], PS[6][:n, :], AF.Sigmoid)
                        tm_group(PS[7], C_GB + 512); b.act(sgb[:n, 512:1024], PS[7][:n, :], AF.Sigmoid)
                        for half in range(2):
                            bank = PS[half]
                            for k in range(8):
                                b.mm(bank[:n, :], ogT[:, k, :n], wpbb[:, k, half * 512:(half + 1) * 512], start=(k == 0), stop=(k == 7))
                            b.tt('dve', gby[:n, half * 512:(half + 1) * 512], bank[:n, :], sgb[:n, half * 512:(half + 1) * 512], ALU.mult)
                        b.dma('pool', GBY[t0:t0 + n, :], gby[:n], wkey='-')
                        tm_group(PS[2], C_GA); b.act(sga[:n, 0:512], PS[2][:n, :], AF.Sigmoid)
                        tm_group(PS[3], C_GA + 512); b.act(sga[:n, 512:1024], PS[3][:n, :], AF.Sigmoid)
                        b.dma('pool', SGA[t0:t0 + n, :], sga[:n], wkey='-')
                        chk('p1merge')
                        tm_group(PS[4], C_VA)
                        b.copy('act', Vb[:n], PS[4][:n, :])
                        b.dma('pool', Vs[t0:t0 + n, :], Vb[:n], wkey='-')
                        orow = None
                        if kind == 'sample':
                            orow = (nks[l, :, :], nvs[l, :, :])
                        elif kind == 'prompt' and r0 >= S - KEEP:
                            o0 = r0 - (S - KEEP)
                            orow = (nkp[l, o0:o0 + n, :], nvp[l, o0:o0 + n, :])
                        if orow is not None:
                            b.copy('dve', Vf[:n], PS[4][:n, :])
                            b.dma('pool', orow[1], Vf[:n], wkey='-')
                            tm_group(PS[5], C_KA)
                            b.copy('dve', Kf[:n], PS[5][:n, :])
                            b.dma('pool', orow[0], Kf[:n], wkey='-')
                        fm_group(PS[6], C_QA, 4, 128)
                        b.act(QTt[:, :, :n], v4(PS[6][:, :4 * n]), AF.Copy, scale=0.125)
                        b.dma('pool', QT[:, :, t0:t0 + n], QTt[:, :, :n], wkey='-')
                        fm_group(PS[7], C_KA, 4, 128)
                        b.copy('dve', KTt[:, :, :n], v4(PS[7][:, :4 * n]))
                        b.dma('pool', KT[:, :, t0:t0 + n], KTt[:, :, :n], wkey='-')
                        if kind == 'prompt' and ti == NT:
                            b.dma('pool', stp[l].rearrange("h k v -> k h v"), Sst['p'][:], wkey='-')
                        chk('p1tile')
                        if kind == 'sample':
                            b.dma('pool', sts[l].rearrange("h k v -> k h v"), Sst['s'][:], wkey='-')
                    b.barrier()

                chk('p1')
                with ExitStack() as p2:
                    wpab = sb("wpab", [64, 8, D], BF16, p2); wob = sb("wob", [128, 8, D], BF16, p2)
                    Mm = [sb("Mm%d" % m, [128, 512], BF16, p2) for m in range(4)]
                    NMm = [sb("NMm%d" % m, [128, 512], BF16, p2) for m in range(4)]
                    QTb = [sb("QTb%d" % i, [128, 4, 512], BF16, p2) for i in range(2)]
                    eb = [sb("eb%d" % i, [128, 512], F32, p2) for i in range(3)]
                    spb = [sb("spb%d" % i, [128, 512], BF16, p2) for i in range(5)]
                    wb = [sb("wb%d" % i, [128, 512], BF16, p2) for i in range(4)]
                    Rb = [sb("Rb%d" % i, [128, 512], BF16, p2) for i in range(2)]
                    sbT = [sb("sbT%d" % i, [64, 8, 512], BF16, p2) for i in range(2)]
                    sgat = sb("sgat", [128, D], F32, p2); gbyt = sb("gbyt", [128, D], F32, p2); hres = sb("hres", [128, D], F32, p2)
                    mT = sb("mT", [128, 8, 128], BF16, p2)
                    with ExitStack() as p2s:
                        stg2 = [sb("stg2%d" % i, [128, 1024], F32, p2s) for i in range(2)]
                        cnt = 0
                        for hh in range(8):
                            load_cast(wpab[:, hh, :], w_pa[l, hh * 64:(hh + 1) * 64, :], 64, D, stg2, cnt); cnt += 1
                            load_cast(wob[:, hh, :], w_o[l, hh * 128:(hh + 1) * 128, :], 128, D, stg2, cnt); cnt += 1
                        for m in range(4):
                            st = stg2[m % 2]
                            b.memset('pool', st[:, :512], 1.0)
                            aff(st[:, :512], st[:, :512], [[1, 512]], ALU.is_gt, -128 * m, -1)
                            b.copy('dve', Mm[m][:], st[:, :512])
                            b.ts('dve', NMm[m][:], st[:, :512], 30000.0, -30000.0, ALU.mult, ALU.add)
                        b.barrier()
                    load_ln(ln1_g[l], ln1_b[l])

                    def run_attention(qblocks, prep):
                        tasks = []
                        for qi, (nq, q0, kbs) in enumerate(qblocks):
                            for hd in range(8):
                                for bi, kb in enumerate(kbs):
                                    tasks.append((qi, hd, bi, kb, bi == 0, bi == len(kbs) - 1))
                        state = {'q': -1}

                        def ensure_q(qi):
                            if state['q'] < qi:
                                nq, q0, _ = qblocks[qi]
                                b.dma('sp', QTb[qi % 2][:, :, :nq], QT[:, :, q0:q0 + nq], rkey='-')
                                state['q'] = qi

                        def qT_ap(qi, hd, nq):
                            po = (hd % 2) * 64
                            return QTb[qi % 2][po:po + 64, hd // 2, :nq]

                        def stage1(ix):
                            qi, hd, bi, kb, first, lastb = tasks[ix]
                            nq = qblocks[qi][0]; nk, m, kTf, vf = kb
                            ensure_q(qi)
                            if qi + 1 < len(qblocks) and hd == 4 and bi == 0:
                                ensure_q(qi + 1)
                            if prep is not None and bi == 0:
                                prep(qi, hd)
                            zA = PS[ix % 2]
                            b.mm(zA[:nk, :nq], kTf(hd), qT_ap(qi, hd, nq))
                            b.act(eb[ix % 3][:nk, :nq], zA[:nk, :nq], AF.Exp)

                        def stage1b(ix):
                            qi, hd, bi, kb, first, lastb = tasks[ix]
                            nq = qblocks[qi][0]; nk, m, kTf, vf = kb
                            b.act(spb[ix % 5][:nk, :nq], eb[ix % 3][:nk, :nq], AF.Ln, bias=1.0)
                            if m is not None:
                                b.tt('dve', spb[ix % 5][:nk, :nq], spb[ix % 5][:nk, :nq], Mm[m][:nk, :nq], ALU.mult)

                        def stage2(ix):
                            qi, hd, bi, kb, first, lastb = tasks[ix]
                            nq = qblocks[qi][0]; nk, m, kTf, vf = kb
                            R = Rb[(qi * 8 + hd) % 2]
                            Bk = PS[2 + ix % 2]
                            sp_ = spb[ix % 5]
                            if first:
                                b.memset('dve', R[:, :nq], 0.0)
                            b.mm(Bk[:nk, :nq], kTf(hd), qT_ap(qi, hd, nq), start=True, stop=False)
                            if not first:
                                b.mm(Bk[:nk, :nq], negOnes[:, :nk], R[:, :nq], start=False, stop=False)
                            if m is not None:
                                b.mm(Bk[:nk, :nq], identB[:nk, :nk], NMm[m][:nk, :nq], start=False, stop=False)
                            b.mm(Bk[:nk, :nq], negT[:nk, :nk], sp_[:nk, :nq], start=False, stop=True)
                            b.act(wb[ix % 4][:nk, :nq], Bk[:nk, :nq], AF.Exp)
                            if not lastb:
                                b.tt('dve', R[:nk, :nq], R[:nk, :nq], sp_[:nk, :nq], ALU.add)

                        def stage3(ix):
                            qi, hd, bi, kb, first, lastb = tasks[ix]
                            nq = qblocks[qi][0]; nk, m, kTf, vf = kb
                            O = PS[4 + (qi * 8 + hd) % 2]
                            b.mm(O[:64, :nq], vf(hd), wb[ix % 4][:nk, :nq], start=first, stop=lastb)
                            if lastb:
                                b.copy('dve', sbT[qi % 2][:, hd, :nq], O[:64, :nq])
                                if hd == 7:
                                    epilogue(qi)

                        def epilogue(qi):
                            nq, q0, _ = qblocks[qi]
                            for s0 in range(0, nq, 128):
                                n = min(128, nq - s0)
                                t0 = q0 + s0
                                b.dma('sp', sgat[:n], SGA[t0:t0 + n, :], rkey='-')
                                b.dma('sp', gbyt[:n], GBY[t0:t0 + n, :], rkey='-')
                                b.dma('sp', hres[:n], Hin[t0:t0 + n, :], rkey='-')
                                for half in range(2):
                                    bank = PS[6 + half]
                                    for hd in range(8):
                                        b.mm(bank[:n, :], sbT[qi % 2][:, hd, s0:s0 + n], wpab[:, hd, half * 512:(half + 1) * 512],
                                             start=(hd == 0), stop=(hd == 7))
                                    b.tt('dve', sgat[:n, half * 512:(half + 1) * 512], bank[:n, :], sgat[:n, half * 512:(half + 1) * 512], ALU.mult)
                                b.tt('dve', sgat[:n], sgat[:n], gbyt[:n], ALU.add)
                                transpose_to(mT, sgat[:n], n, False, (PS[6], PS[7]))
                                for half in range(2):
                                    bank = PS[6 + half]
                                    for k in range(8):
                                        b.mm(bank[:n, :], mT[:, k, :n], wob[:, k, half * 512:(half + 1) * 512], start=(k == 0), stop=(k == 7))
                                    b.stt('dve', hres[:n, half * 512:(half + 1) * 512], hres[:n, half * 512:(half + 1) * 512], ALPHA,
                                          bank[:n, :], ALU.mult, ALU.add)
                                layernorm(gbyt[:n], hres[:n], n)
                                b.dma('pool', X1[t0:t0 + n, :], gbyt[:n], wkey='-')

                        NTK = len(tasks)
                        for i in range(NTK + 4):
                            if i < NTK:
                                stage1(i)
                            if 0 <= i - 1 < NTK:
                                stage1b(i - 1)
                            if 0 <= i - 3 < NTK:
                                stage2(i - 3)
                            if 0 <= i - 4 < NTK:
                                stage3(i - 4)

                    with ExitStack() as p2a:
                        cks = sb("cks", [128, 4, P], BF16, p2a); cvs = sb("cvs", [128, P // 128, 512], BF16, p2a)
                        KTm = sb("KTm", [128, 4, 32], BF16, p2a); Vm = sb("Vm", [16, 2, 512], BF16, p2a)
                        with ExitStack() as p2as:
                            stg2 = [sb("stg2a%d" % i, [128, 2048], F32, p2as) for i in range(2)]
                            cnt = 0
                            for c in range(4):
                                for c0 in range(0, P, 2048):
                                    cw = min(2048, P - c0)
                                    load_cast(cks[:, c, c0:c0 + cw], ckT[l, :, c, c0:c0 + cw], 128, cw, stg2, cnt); cnt += 1
                            for j in range(0, P // 128, 4):
                                jn = min(4, P // 128 - j)
                                st = stg2[cnt % 2]
                                b.dma('sp', st[:, :jn * 512].rearrange("p (j f) -> p j f", j=jn),
                                      cvv[l, j * 128:(j + jn) * 128, :].rearrange("(j p) f -> p j f", p=128))
                                b.copy(engs3[cnt % 3], cvs[:, j:j + jn, :], st[:, :jn * 512].rearrange("p (j f) -> p j f", j=jn)); cnt += 1
                            b.barrier()
                        b.dma('sp', KTm[:, :, 0:16], KT[:, :, 0:16], rkey='-')
                        b.dma('sp', KTm[:, :, 16:32], KT[:, :, 16 + S:TOK], rkey='-')
                        b.dma('sp', Vm[:, 0, :], Vs[0:16, :], rkey='-')
                        b.dma('sp', Vm[:, 1, :], Vs[16 + S:TOK, :], rkey='-')

                        def kT_small(c0):
                            return lambda hd: KTm[(hd % 2) * 64:(hd % 2) * 64 + 64, hd // 2, c0:c0 + 16]

                        def v_small(t):
                            return lambda hd: Vm[:16, t, hd * 64:(hd + 1) * 64]

                        def kT_cache(j):
                            return lambda hd: cks[(hd % 2) * 64:(hd % 2) * 64 + 64, hd // 2, j * 128:(j + 1) * 128]

                        def v_cache(j):
                            return lambda hd: cvs[:, j, hd * 64:(hd + 1) * 64]

                        kb_meta = (16, None, kT_small(0), v_small(0))
                        qbA = [(16, 0, [(16, 0, kT_small(0), v_small(0))])]
                        kbs = [(16, 0, kT_small(16), v_small(1))]
                        for j in range(P // 128 - 1, -1, -1):
                            kbs.append((128, None, kT_cache(j), v_cache(j)))
                        kbs.append(kb_meta)
                        qbA.append((16, 16 + S, kbs))
                        run_attention(qbA, None)
                        b.barrier()

                    chk('p2a')
                    with ExitStack() as p2b:
                        KTp = [sb("KTp%d" % i, [128, S + 16], BF16, p2b) for i in range(2)]
                        Vp = [sb("Vp%d" % i, [128, NT + 1, 128], BF16, p2b) for i in range(2)]
                        pst = {'n': -1}

                        def load_pair(pc):
                            if pst['n'] >= pc or pc >= 4 * (S // 512):
                                return
                            qi, c = pc // 4, pc % 4
                            ntok = 16 + 512 * (qi + 1)
                            ntile = 4 * (qi + 1)
                            b.dma('sp', KTp[pc % 2][:, :ntok], KT[:, c, 0:ntok], rkey='-')
                            b.dma('sp', Vp[pc % 2][:16, 0, :], Vs[0:16, c * 128:(c + 1) * 128], rkey='-')
                            for j0 in range(0, ntile, 8):
                                jn = min(8, ntile - j0)
                                b.dma('sp', Vp[pc % 2][:, 1 + j0:1 + j0 + jn, :],
                                      Vs[16 + 128 * j0:16 + 128 * (j0 + jn), c * 128:(c + 1) * 128].rearrange("(j p) f -> p j f", p=128), rkey='-')
                            pst['n'] = pc

                        def prep(qi, hd):
                            pc = qi * 4 + hd // 2
                            if hd % 2 == 0:
                                load_pair(pc)
                            else:
                                load_pair(pc + 1)

                        def kT_p(qi, c0, nk):
                            return lambda hd: KTp[(qi * 4 + hd // 2) % 2][(hd % 2) * 64:(hd % 2) * 64 + 64, c0:c0 + nk]

                        def v_p(qi, vt, nk):
                            return lambda hd: Vp[(qi * 4 + hd // 2) % 2][:nk, vt, (hd % 2) * 64:(hd % 2) * 64 + 64]

                        qbB = []
                        for i in range(S // 512):
                            kbs = []
                            for m in range(3, -1, -1):
                                j = 4 * i + m
                                kbs.append((128, m, kT_p(i, 16 + 128 * j, 128), v_p(i, 1 + j, 128)))
                            for j in range(4 * i - 1, -1, -1):
                                kbs.append((128, None, kT_p(i, 16 + 128 * j, 128), v_p(i, 1 + j, 128)))
                            kbs.append((16, None, kT_p(i, 0, 16), v_p(i, 0, 16)))
                            qbB.append((512, 16 + 512 * i, kbs))
                        run_attention(qbB, prep)
                        b.barrier()
                    b.barrier()

                chk('p2')
                with ExitStack() as p3:
                    wqb = sb("wqb", [128, 8, 2048], BF16, p3)
                    skTs = sb("skTs", [128, 2, 128], F32, p3)
                    with ExitStack() as p3s:
                        stg3 = [sb("stg3%d" % i, [128, 2048], F32, p3s) for i in range(2)]
                        for k in range(8):
                            load_cast(wqb[:, k, :], wq[l, k * 128:(k + 1) * 128, :], 128, 2048, stg3, k)
                        b.barrier()
                    b.dma('sp', skTs[:], skT[l].rearrange("p d k -> d p k"))
                    load_ln(ln2_g[l], ln2_b[l])
                    x1 = [sb("x1_%d" % i, [128, D], F32, p3) for i in range(2)]
                    x1T = sb("x1T", [128, 8, 128], BF16, p3)
                    qT = sb("qT", [128, 16, 128], F32, p3)
                    Ssb = sb("Ssb", [128, 16, 128], F32, p3); S2 = sb("S2", [128, 16, 128], F32, p3)
                    sv = sb("sv", [128, 16, 16], F32, p3); si = sb("si", [128, 16, 16], U32, p3); sif = sb("sif", [128, 16, 16], F32, p3)
                    cand = sb("cand", [128, 8, 256], F32, p3); cand2 = sb("cand2", [128, 8, 256], F32, p3)
                    cv = sb("cv", [128, 8, 16], F32, p3); ci = sb("ci", [128, 8, 16], U32, p3)
                    ii = sb("ii", [128, 128], U32, p3); jj = sb("jj", [128, 128], U32, p3)
                    iif = sb("iif", [128, 128], F32, p3); jjf = sb("jjf", [128, 128], F32, p3)
                    eq = sb("eq", [128, 128, 16], F32, p3)
                    smx = sb("smx", [128, 8], F32, p3); sme = sb("sme", [128, 8, 16], F32, p3); smz = sb("smz", [128, 8], F32, p3)
                    selt = sb("selt", [128, 3, 128], F32, p3)
                    i1f = selt[:, 0, :]; i2f = selt[:, 1, :]; gg = selt[:, 2, :]

                    def top16(vals, vals2, ov, oi):
                        b.op('dve', lambda: vec.max(out=ov[:, 0:8], in_=vals), [vals], [ov])
                        b.op('dve', lambda: vec.max_index(out=oi[:, 0:8], in_max=ov[:, 0:8], in_values=vals), [ov, vals], [oi])
                        b.op('dve', lambda: vec.match_replace(out=vals2, in_to_replace=ov[:, 0:8], in_values=vals, imm_value=-1e30),
                             [ov, vals], [vals2])
                        b.op('dve', lambda: vec.max(out=ov[:, 8:16], in_=vals2), [vals2], [ov])
                        b.op('dve', lambda: vec.max_index(out=oi[:, 8:16], in_max=ov[:, 8:16], in_values=vals2), [ov, vals2], [oi])

                    b.dma('sp', x1[0][:tiles[0][0]], X1[0:tiles[0][0], :], rkey='-')
                    for ti, (n, t0, kind, r0) in enumerate(tiles):
                        x_ = x1[ti % 2]
                        if ti + 1 < len(tiles):
                            n1, t1 = tiles[ti + 1][0], tiles[ti + 1][1]
                            b.dma('sp', x1[(ti + 1) % 2][:n1], X1[t1:t1 + n1, :], rkey='-')
                        transpose_to(x1T, x_[:n], n, False, (PS[0], PS[1]))
                        for c in range(16):
                            bank = PS[2 + c // 4]
                            for k in range(8):
                                b.mm(bank[:, (c % 4) * n:(c % 4 + 1) * n], wqb[:, k, c * 128:(c + 1) * 128], x1T[:, k, :n],
                                     start=(k == 0), stop=(k == 7))
                            if c % 4 == 3:
                                b.copy('act' if (c // 4) % 2 else 'dve', qT[:, c - 3:c + 1, :n],
                                       bank[:, :4 * n].rearrange("p (c t) -> p c t", c=4))
                        for c in range(16):
                            bank = PS[6 + (c // 4) % 2]
                            b.mm(bank[:n, (c % 4) * 128:(c % 4 + 1) * 128], qT[:, c, :n], skTs[:, c % 2, :])
                            if c % 4 == 3:
                                b.copy('act' if (c // 4) % 2 else 'dve', Ssb[:n, c - 3:c + 1, :],
                                       bank[:n, :].rearrange("p (c k) -> p c k", c=4))
                        for c in range(16):
                            top16(Ssb[:n, c, :], S2[:n, c, :], sv[:n, c, :], si[:n, c, :])
                        b.copy('dve', sif[:n], si[:n])
                        for hd in range(8):
                            b.tt('dve', cand[:n, hd, :].rearrange("p (i j) -> p i j", i=16),
                                 sv[:n, 2 * hd, :].unsqueeze(2).to_broadcast([n, 16, 16]),
                                 sv[:n, 2 * hd + 1, :].unsqueeze(1).to_broadcast([n, 16, 16]), ALU.add)
                        for hd in range(8):
                            top16(cand[:n, hd, :], cand2[:n, hd, :], cv[:n, hd, :], ci[:n, hd, :])
                        cif = ci[:n].rearrange("p h k -> p (h k)")
                        b.op('dve', lambda: vec.tensor_single_scalar(out=ii[:n], in_=cif, scalar=4, op=ALU.logical_shift_right), [ci], [ii])
                        b.op('dve', lambda: vec.tensor_single_scalar(out=jj[:n], in_=cif, scalar=15, op=ALU.bitwise_and), [ci], [jj])
                        b.copy('dve', iif[:n], ii[:n]); b.copy('dve', jjf[:n], jj[:n])
                        sif4 = sif[:n].rearrange("p (h two) k -> p h two k", two=2)
                        for (srcf, half, dst) in ((iif, 0, i1f), (jjf, 1, i2f)):
                            b.tt('dve', eq[:n], srcf[:n].unsqueeze(2).to_broadcast([n, 128, 16]),
                                 iota16[:n].unsqueeze(1).to_broadcast([n, 128, 16]), ALU.is_equal)
                            eq4 = eq[:n].rearrange("p (h c) i -> p h c i", h=8)
                            b.tt('dve', eq4, eq4, sif4[:, :, half, :].unsqueeze(2).to_broadcast([n, 8, 16, 16]), ALU.mult)
                            b.red('dve', dst[:n], eq[:n], ALU.add)
                        b.red('dve', smx[:n], cv[:n], ALU.max)
                        b.tt('dve', sme[:n], cv[:n], smx[:n].unsqueeze(2).to_broadcast([n, 8, 16]), ALU.subtract)
                        b.act(sme[:n], sme[:n], AF.Exp)
                        b.red('dve', smz[:n], sme[:n], ALU.add)
                        b.op('dve', lambda: vec.reciprocal(out=smz[:n], in_=smz[:n]), [smz], [smz])
                        b.tt('dve', gg[:n].rearrange("p (h k) -> p h k", h=8), sme[:n], smz[:n].unsqueeze(2).to_broadcast([n, 8, 16]), ALU.mult)
                        chk('p3top')
                        b.dma('pool', SEL[t0:t0 + n, :, :], selt[:n], wkey='-')
                    b.barrier()

                chk('p3a')
                with ExitStack() as p3b:
                    Gall = sb("Gall", [128, 384, 128], BF16, p3b)
                    uTs = [sb("uTs%d" % i, [128, 8, 512], BF16, p3b) for i in range(2)]
                    vSs = [sb("vSs%d" % i, [128, 4, 1024], BF16, p3b) for i in range(2)]
                    xg = [sb("xg%d" % i, [128, D], F32, p3b) for i in range(3)]
                    xgT = sb("xgT", [128, 8, 384], BF16, p3b)
                    selg = sb("selg", [128, 3, 128], F32, p3b); selT = sb("selT", [128, 3, 384], F32, p3b)
                    TB = 16
                    Call = [sb("Call%d" % i, [128, TB, 128], BF16, p3b) for i in range(2)]
                    Wall = [sb("Wall%d" % i, [128, TB, 128], BF16, p3b) for i in range(2)]
                    selTb = sb("selTb", [128, 3, 384], BF16, p3b)
                    iotab = sb("iotab", [128, 128], BF16, p3b)
                    b.copy('dve', iotab[:], iota128[:])
                    gel = [sb("gel%d" % i, [128, 384], F32, p3b) for i in range(2)]
                    AT = [sb("AT%d" % i, [128, 384], BF16, p3b) for i in range(3)]
                    acc = sb("acc", [128, D], F32, p3b); x2 = sb("x2", [128, D], F32, p3b)
                    groups = [[tiles[0]]] + [tiles[1 + 3 * g: min(4 + 3 * g, NT + 1)] for g in range((NT + 2) // 3)] + [[tiles[-1]]]
                    NSC = NEXP // 512
                    wst = {'n': -1}

                    def load_w(gsc):
                        if wst['n'] >= gsc or gsc >= len(groups) * NSC:
                            return
                        sc = gsc % NSC
                        b.dma('sp', uTs[gsc % 2][:], UT[l, :, sc * 512:(sc + 1) * 512].rearrange("(k p) e -> p k e", p=128), rkey='-')
                        b.dma('sp', vSs[gsc % 2][:], VB[l, sc * 512:(sc + 1) * 512, :].rearrange("(c p) d -> p c d", p=128), rkey='-')
                        wst['n'] = gsc

                    for gi_, grp in enumerate(groups):
                        load_w(gi_ * NSC)
                        T = sum(t[0] for t in grp)
                        cs = 0
                        offs = []
                        for s_, (n, t0, kind, r0) in enumerate(grp):
                            offs.append(cs)
                            b.dma('sp', xg[s_][:n], X1[t0:t0 + n, :], rkey='-')
                            b.dma('sp', selg[:n], SEL[t0:t0 + n, :, :], rkey='-')
                            transpose_to(xgT[:, :, cs:cs + n], xg[s_][:n], n, False, (PS[6], PS[7]))
                            for w_ in range(3):
                                b.mm(PS[6][:, w_ * n:(w_ + 1) * n], selg[:n, w_, :], identF[:n, :n])
                            b.copy('dve', selT[:, :, cs:cs + n], PS[6][:, 0:3 * n].rearrange("p (w t) -> p w t", w=3))
                            cs += n
                        b.copy('act', selTb[:, :, :T], selT[:, :, :T])
                        for tk in range(T):
                            if tk % TB == 0:
                                bi_ = (tk // TB) % 2
                                iob = iotab[:].unsqueeze(1).to_broadcast([128, TB, 128])
                                b.tt('dve', Call[bi_][:], iob, selTb[:, 1, tk:tk + TB].unsqueeze(2).to_broadcast([128, TB, 128]), ALU.is_equal)
                                b.tt('dve', Wall[bi_][:], iob, selTb[:, 0, tk:tk + TB].unsqueeze(2).to_broadcast([128, TB, 128]), ALU.is_equal)
                                b.tt('dve', Wall[bi_][:], Wall[bi_][:], selTb[:, 2, tk:tk + TB].unsqueeze(2).to_broadcast([128, TB, 128]), ALU.mult)
                            c_ = Call[(tk // TB) % 2][:, tk % TB, :]; w_ = Wall[(tk // TB) % 2][:, tk % TB, :]
                            gp = PS[6 + (tk // 4) % 2]
                            b.mm(gp[:, (tk % 4) * 128:(tk % 4 + 1) * 128], c_, w_)
                            if tk % 4 == 3:
                                b.copy('act', Gall[:, tk - 3:tk + 1, :], gp[:, :].rearrange("p (t i) -> p t i", t=4))

                        def Hmm(ch):
                            gsc = gi_ * NSC + ch // 4
                            for k in range(8):
                                b.mm(PS[6 + ch % 2][:, :T], uTs[gsc % 2][:, k, (ch % 4) * 128:(ch % 4 + 1) * 128], xgT[:, k, :T],
                                     start=(k == 0), stop=(k == 7))

                        Hmm(0)
                        for ch in range(128):
                            gsc = gi_ * NSC + ch // 4
                            if ch % 4 == 0:
                                load_w(gsc + 1)
                            if ch + 1 < 128:
                                Hmm(ch + 1)
                            b.act(gel[ch % 2][:, :T], PS[6 + ch % 2][:, :T], AF.Gelu)
                            b.tt('dve', AT[ch % 3][:, :T], gel[ch % 2][:, :T], Gall[:, :T, ch], ALU.mult)
                            for s_, (n, t0, kind, r0) in enumerate(grp):
                                for half in range(2):
                                    b.mm(PS[2 * s_ + half][:n, :], AT[ch % 3][:, offs[s_]:offs[s_] + n],
                                         vSs[gsc % 2][:, ch % 4, half * 512:(half + 1) * 512], start=(ch == 0), stop=(ch == 127))
                        for s_, (n, t0, kind, r0) in enumerate(grp):
                            for half in range(2):
                                b.stt('dve', acc[:n, half * 512:(half + 1) * 512], xg[s_][:n, half * 512:(half + 1) * 512], ALPHA,
                                      PS[2 * s_ + half][:n, :], ALU.mult, ALU.add)
                            layernorm(x2[:n], acc[:n], n)
                            if not last:
                                b.dma('pool', Hout[t0:t0 + n, :], x2[:n], wkey='-')
                            elif kind == 'prompt':
                                b.dma('pool', yp[r0:r0 + n, :], x2[:n], wkey='-')
                            elif kind == 'sample':
                                b.dma('pool', ys[:, :], x2[:n], wkey='-')
                    b.barrier()
        except _Stop:
            pass
        b.stopped = False
        b.barrier()
    return nc


def make_in_maps(inputs, S, P, ncores):
    f = lambda a: np.ascontiguousarray(np.asarray(a, dtype=np.float32))
    ck = np.asarray(inputs['cache_k'], dtype=np.float32)
    cvv = np.asarray(inputs['cache_v'], dtype=np.float32)
    common = {
        "meta": f(inputs['meta']), "ln_in_g": f(inputs['ln_in_g']).reshape(1, D), "ln_in_b": f(inputs['ln_in_b']).reshape(1, D),
        "w_in": f(inputs['w_in']), "w_gk2": f(inputs['w_gk2']), "b_gk": f(inputs['b_gk']), "gng": f(inputs['gla_norm_g']),
        "w_pa": f(inputs['w_pa']), "w_pb": f(inputs['w_pb']), "w_o": f(inputs['w_o']),
        "ln1_g": f(inputs['ln1_g']), "ln1_b": f(inputs['ln1_b']), "wq": f(inputs['peer_wq']),
        "skT": f(np.asarray(inputs['peer_subkeys']).transpose(0, 1, 3, 2)),
        "puT": f(np.asarray(inputs['peer_u']).transpose(0, 2, 1)), "pv": f(inputs['peer_v']), "ln2_g": f(inputs['ln2_g']), "ln2_b": f(inputs['ln2_b']),
    }
    maps = []
    for c in range(ncores):
        m = dict(common)
        m["xp"] = f(inputs['x_prompt'][c]); m["xs"] = f(inputs['x_sample'][c])
        m["ckT"] = f(ck[:, c].reshape(DEPTH, P, 4, 2, 64).transpose(0, 3, 4, 2, 1).reshape(DEPTH, 128, 4, P))
        m["cv"] = f(cvv[:, c].reshape(DEPTH, P, 512))
        m["sg"] = f(np.asarray(inputs['state_gla'])[:, c])
        maps.append(m)
    return maps


def assemble(results, S, P):
    KEEP = min(S, P)
    nb = len(results)
    yp = np.stack([r["yp"] for r in results]); ys = np.stack([r["ys"] for r in results])
    nkp = np.stack([r["nkp"] for r in results], axis=1).reshape(DEPTH, nb, KEEP, 8, 64)
    nvp = np.stack([r["nvp"] for r in results], axis=1).reshape(DEPTH, nb, KEEP, 8, 64)
    stp = np.stack([r["stp"] for r in results], axis=1)
    nks = np.stack([r["nks"] for r in results], axis=1).reshape(DEPTH, nb, TS, 8, 64)
    nvs = np.stack([r["nvs"] for r in results], axis=1).reshape(DEPTH, nb, TS, 8, 64)
    sts = np.stack([r["sts"] for r in results], axis=1)
    return tuple(np.ascontiguousarray(a, dtype=np.float32) for a in (yp, ys, nkp, nvp, stp, nks, nvs, sts))


def kernel(**inputs):
    S = inputs['x_prompt'].shape[1]; P = inputs['cache_k'].shape[2]
    nb = inputs['x_prompt'].shape[0]
    nc = build(S, P)
    maps = make_in_maps(inputs, S, P, nb)
    res = run_bass_kernel_spmd(nc, maps, core_ids=list(range(nb)))
    return assemble(res.results, S, P)
```

```python
import numpy as np
from contextlib import ExitStack
import concourse.bass as bass
import concourse.mybir as mybir
from concourse.bass_utils import run_bass_kernel_spmd

F32 = mybir.dt.float32; BF16 = mybir.dt.bfloat16; I32 = mybir.dt.int32; U32 = mybir.dt.uint32
AF = mybir.ActivationFunctionType; ALU = mybir.AluOpType; AX = mybir.AxisListType

D = 1024; DEPTH = 2; NMETA = 16; TS = 16
INW = 6672
C_QA, C_KA, C_VA, C_QB, C_KB, C_VB, C_RB, C_GLR, C_GA, C_GB = 0, 512, 1024, 1536, 2048, 2560, 3584, 4608, 4624, 5648
ALPHA = float((2 * DEPTH) ** 0.25)
LN_EPS = 1e-5; RMS_EPS = 1e-6
NEXP = 16384
NDS = 96


class Bld:
    def __init__(self, nc, es):
        self.nc = nc
        self.E = {'pe': nc.tensor, 'act': nc.scalar, 'dve': nc.vector, 'pool': nc.gpsimd, 'sp': nc.sync}
        self.sem = {k: es.enter_context(nc.semaphore('s_' + k)) for k in self.E}
        self.cnt = {k: 0 for k in self.E}
        self.seen = {k: {} for k in self.E}
        self.dsem = [es.enter_context(nc.semaphore('d%d' % i)) for i in range(NDS)]
        self.dcnt = [0] * NDS
        self.dnext = 0
        self.dq = {}
        self.lastw = {}
        self.readers = {}
        self.stopped = False
        self.ps = [es.enter_context(nc.psum_tensor('ps%d' % i, [128, 512], F32)) for i in range(8)]

    @staticmethod
    def key(a):
        return a if isinstance(a, str) else (a.tensor.name if hasattr(a, "tensor") else a.name)

    def _wait(self, eng, toks):
        best = {}
        for t in toks:
            if t is None:
                continue
            sk, h, v, te = t
            if te == 'pe' and eng == 'pe':
                continue
            if self.seen[eng].get(sk, 0) >= v:
                continue
            if sk not in best or best[sk][2] < v:
                best[sk] = t
        for sk, (_, h, v, te) in best.items():
            self.E[eng].wait_ge(h, v)
            self.seen[eng][sk] = v

    def _sync(self, eng, rk, wk):
        toks = []
        for k in rk:
            toks.append(self.lastw.get(k))
            if k.startswith('ps'):
                toks.extend(t for t in self.readers.get(k, {}).values() if t[3] != eng)
        for k in wk:
            toks.append(self.lastw.get(k))
            toks.extend(self.readers.get(k, {}).values())
        self._wait(eng, toks)

    def _record(self, tok, rk, wk):
        for k in wk:
            self.lastw[k] = tok
            self.readers[k] = {}
        for k in rk:
            if k in wk:
                continue
            self.readers.setdefault(k, {})[tok[0]] = tok

    def op(self, eng, fn, reads, writes):
        if self.stopped:
            return None
        rk = {self.key(a) for a in reads if not isinstance(a, (int, float))}
        wk = {self.key(a) for a in writes}
        self._sync(eng, rk, wk)
        inst = fn()
        inst.then_inc(self.sem[eng], 1)
        self.cnt[eng] += 1
        self._record((eng, self.sem[eng], self.cnt[eng], eng), rk, wk)
        return inst

    def dma(self, q, out, in_, rkey=None, wkey=None, fn=None):
        if self.stopped:
            return None
        rk = {self.key(in_) if rkey is None else rkey}
        wk = {self.key(out) if wkey is None else wkey}
        rk.discard('-'); wk.discard('-')
        self._sync(q, rk, wk)
        inst = fn() if fn is not None else self.E[q].dma_start(out=out, in_=in_)
        i = self._dsel(q)
        self.dcnt[i] += 16
        inst.then_inc(self.dsem[i], 16)
        self._record((('d', i), self.dsem[i], self.dcnt[i], None), rk, wk)

    def _dsel(self, q):
        half = NDS // 2
        base = 0 if q == 'sp' else half
        j = self.dq.get(q, 0)
        self.dq[q] = (j + 1) % half
        return base + j

    def gather(self, out, table, idx, extra_reads=()):
        if self.stopped:
            return None
        rk = {self.key(idx)} | {self.key(a) for a in extra_reads}
        wk = {self.key(out)}
        self._sync('pool', rk, wk)
        inst = self.nc.gpsimd.indirect_dma_start(out=out, out_offset=None, in_=table,
                                                 in_offset=bass.IndirectOffsetOnAxis(ap=idx, axis=0))
        i = self._dsel('pool')
        self.dcnt[i] += 16
        inst.then_inc(self.dsem[i], 16)
        self._record((('d', i), self.dsem[i], self.dcnt[i], None), rk, wk)

    def barrier(self):
        if self.stopped:
            return None
        toks = [(k, self.sem[k], self.cnt[k], None) for k in self.E if self.cnt[k] > 0]
        toks += [(('d', i), self.dsem[i], self.dcnt[i], None) for i in range(NDS) if self.dcnt[i] > 0]
        for e in self.E:
            self._wait(e, [t for t in toks if t[0] != e])
        self.lastw = {}
        self.readers = {}

    def mm(self, out, lhsT, rhs, start=True, stop=True):
        rd = [lhsT, rhs] + ([] if start else [out])
        return self.op('pe', lambda: self.nc.tensor.matmul(out, lhsT=lhsT, rhs=rhs, start=start, stop=stop), rd, [out])

    def act(self, out, in_, func, bias=None, scale=None, accum_out=None):
        kw = {}
        rd = [in_]
        if bias is not None:
            kw['bias'] = bias
            rd.append(bias)
        if scale is not None:
            kw['scale'] = scale
            rd.append(scale)
        return self.op('act', lambda: self.nc.scalar.activation(out=out, in_=in_, func=func, **kw), rd, [out])

    def tt(self, eng, out, in0, in1, op):
        return self.op(eng, lambda: self.E[eng].tensor_tensor(out=out, in0=in0, in1=in1, op=op), [in0, in1], [out])

    def ts(self, eng, out, in0, s1, s2, op0, op1=None):
        if op1 is None:
            return self.op(eng, lambda: self.E[eng].tensor_scalar(out=out, in0=in0, scalar1=s1, scalar2=None, op0=op0),
                           [in0, s1], [out])
        return self.op(eng, lambda: self.E[eng].tensor_scalar(out=out, in0=in0, scalar1=s1, scalar2=s2, op0=op0, op1=op1),
                       [in0, s1, s2], [out])

    def stt(self, eng, out, in0, scalar, in1, op0, op1):
        return self.op(eng, lambda: self.E[eng].scalar_tensor_tensor(out=out, in0=in0, scalar=scalar, in1=in1, op0=op0, op1=op1),
                       [in0, scalar, in1], [out])

    def copy(self, eng, out, in_):
        if eng == 'act':
            return self.act(out, in_, AF.Copy)
        return self.op(eng, lambda: self.E[eng].tensor_copy(out=out, in_=in_), [in_], [out])

    def memset(self, eng, ap, val):
        return self.op(eng, lambda: self.E[eng].memset(ap, val), [], [ap])

    def red(self, eng, out, in_, op, axis=AX.X):
        return self.op(eng, lambda: self.E[eng].tensor_reduce(out=out, in_=in_, axis=axis, op=op), [in_], [out])


class _Stop(Exception):
    pass


def build(S, P, dbg=False, nlayers=DEPTH, stop_at=None):
    NT = S // 128
    TOK = S + 32
    KEEP = min(S, P)
    nc = bass.Bass("TRN2", target_bir_lowering=False)

    def din(name, shape, dt=F32):
        return nc.dram_tensor(name, shape, dt, kind="ExternalInput").ap()

    def dout(name, shape, dt=F32):
        return nc.dram_tensor(name, shape, dt, kind="ExternalOutput").ap()

    def dscr(name, shape, dt=F32):
        return nc.dram_tensor(name, shape, dt, kind=("ExternalOutput" if dbg else "Internal")).ap()

    xp = din("xp", [S, D]); xs = din("xs", [TS, D]); meta = din("meta", [NMETA, D])
    ckT = din("ckT", [DEPTH, 128, 4, P]); cvv = din("cv", [DEPTH, P, 512]); sg = din("sg", [DEPTH, 4, 128, 256])
    ln_in_g = din("ln_in_g", [1, D]); ln_in_b = din("ln_in_b", [1, D])
    w_in = din("w_in", [DEPTH, D, INW]); w_gk2 = din("w_gk2", [DEPTH, 16, 512]); b_gk = din("b_gk", [DEPTH, 512])
    gng = din("gng", [DEPTH, 256]); w_pa = din("w_pa", [DEPTH, 512, D]); w_pb = din("w_pb", [DEPTH, D, D])
    w_o = din("w_o", [DEPTH, D, D]); ln1_g = din("ln1_g", [DEPTH, D]); ln1_b = din("ln1_b", [DEPTH, D])
    wq = din("wq", [DEPTH, D, 2048]); skT = din("skT", [DEPTH, 2, 128, 128])
    puT = din("puT", [DEPTH, D, NEXP]); pv = din("pv", [DEPTH, NEXP, D])
    ln2_g = din("ln2_g", [DEPTH, D]); ln2_b = din("ln2_b", [DEPTH, D])

    yp = dout("yp", [S, D]); ys = dout("ys", [TS, D])
    nkp = dout("nkp", [DEPTH, KEEP, 512]); nvp = dout("nvp", [DEPTH, KEEP, 512]); stp = dout("stp", [DEPTH, 4, 128, 256])
    nks = dout("nks", [DEPTH, TS, 512]); nvs = dout("nvs", [DEPTH, TS, 512]); sts = dout("sts", [DEPTH, 4, 128, 256])

    Hs = [dscr("H0", [TOK, D]), dscr("H1", [TOK, D])]
    UT = dscr("UT", [DEPTH, D, NEXP], BF16); VB = dscr("VB", [DEPTH, NEXP, D], BF16); SEL = dscr("SEL", [TOK, 3, 128])
    QT = dscr("QT", [128, 4, TOK], BF16); KT = dscr("KT", [128, 4, TOK], BF16); Vs = dscr("Vs", [TOK, 512], BF16)
    GBY = dscr("GBY", [TOK, D]); SGA = dscr("SGA", [TOK, D]); X1 = dscr("X1", [TOK, D])

    tiles = [(16, 0, 'meta', 0)] + [(128, 16 + 128 * j, 'prompt', 128 * j) for j in range(NT)] + [(16, 16 + S, 'sample', 0)]

    with ExitStack() as es:
        b = Bld(nc, es)
        PS = b.ps
        pool = nc.gpsimd
        vec = nc.vector

        cur_ti = [0]

        def chk(name):
            if stop_at == name or stop_at == '%s@%d' % (name, cur_ti[0]):
                b.stopped = True

        uid = [0]

        def sb(name, shape, dt=F32, stack=es):
            uid[0] += 1
            return stack.enter_context(nc.sbuf_tensor("%s_%d" % (name, uid[0]), shape, dt))

        def aff(out, in_, pattern, cmp, base, cm, fill=0.0):
            b.op('pool', lambda: pool.affine_select(out=out, in_=in_, pattern=pattern, compare_op=cmp, fill=fill,
                                                    base=base, channel_multiplier=cm), [in_], [out])

        identF = sb("identF", [128, 128]); identB = sb("identB", [128, 128], BF16)
        maskLE = sb("maskLE", [128, 128]); triInc = sb("triInc", [128, 128]); triGT = sb("triGT", [128, 128])
        negT = sb("negT", [128, 128], BF16); negOnes = sb("negOnes", [128, 128], BF16)
        iota16 = sb("iota16", [128, 16]); onesrow = sb("onesrow", [1, 128])
        lng = sb("lng", [128, D]); lnb = sb("lnb", [128, D])
        ln_s1 = sb("ln_s1", [128, 1]); ln_s2 = sb("ln_s2", [128, 1]); ln_xc = sb("ln_xc", [128, D]); ln_j = sb("ln_j", [128, D])
        b.memset('pool', identF[:], 1.0)
        aff(identF[:], identF[:], [[-1, 128]], ALU.is_equal, 0, 1)
        b.copy('dve', identB[:], identF[:])
        b.memset('pool', maskLE[:], 1.0)
        aff(maskLE[:], maskLE[:], [[1, 128]], ALU.is_ge, 0, -1)
        b.ts('dve', triInc[:], maskLE[:], -1.0 / 16.0, None, ALU.mult)
        b.memset('pool', triGT[:], -1.0 / 16.0)
        aff(triGT[:], triGT[:], [[-1, 128]], ALU.is_gt, 0, 1)
        b.memset('pool', ln_j[:, 0:128], -1.0)
        aff(ln_j[:, 0:128], ln_j[:, 0:128], [[-1, 128]], ALU.is_ge, 0, 1)
        b.copy('dve', negT[:], ln_j[:, 0:128])
        b.memset('pool', negOnes[:], -1.0)
        b.op('pool', lambda: pool.iota(iota16[:], pattern=[[1, 16]], base=0, channel_multiplier=0,
                                       allow_small_or_imprecise_dtypes=True), [], [iota16])
        b.memset('pool', onesrow[:], 1.0)

        def load_ln(g_ap, b_ap):
            b.dma('sp', lng[:], g_ap.partition_broadcast(128))
            b.dma('sp', lnb[:], b_ap.partition_broadcast(128))

        def rsqrt_inplace(t):
            b.act(t, t, AF.Sqrt)
            b.op('dve', lambda: vec.reciprocal(out=t, in_=t), [t], [t])

        def layernorm(out, x, n):
            b.red('dve', ln_s1[:n], x, ALU.add)
            b.ts('dve', ln_s1[:n], ln_s1[:n], -1.0 / D, None, ALU.mult)
            b.ts('dve', ln_xc[:n], x, ln_s1[:n, 0:1], None, ALU.add)
            b.memset('dve', ln_s2[:n], 0.0)
            b.op('dve', lambda: vec.scalar_tensor_tensor(out=ln_j[:n], in0=ln_xc[:n], scalar=1.0, in1=ln_xc[:n],
                                                         op0=ALU.mult, op1=ALU.mult, accum_out=ln_s2[:n]),
                 [ln_xc], [ln_j, ln_s2])
            b.ts('dve', ln_s2[:n], ln_s2[:n], 1.0 / D, LN_EPS, ALU.mult, ALU.add)
            rsqrt_inplace(ln_s2[:n])
            b.stt('dve', ln_j[:n], ln_xc[:n], ln_s2[:n, 0:1], lng[:n], ALU.mult, ALU.mult)
            b.tt('dve', out, ln_j[:n], lnb[:n], ALU.add)

        def transpose_to(dstT, src, n, bf_src, banks, eng0='dve', eng1='act'):
            ident = identB if bf_src else identF
            for half in range(2):
                bank = banks[half]
                for c in range(4):
                    k = half * 4 + c
                    b.mm(bank[:, c * n:(c + 1) * n], src[:, k * 128:(k + 1) * 128], ident[:n, :n])
                b.copy(eng0 if half == 0 else eng1, dstT[:, half * 4:(half + 1) * 4, :n],
                       bank[:, 0:4 * n].rearrange("p (c t) -> p c t", c=4))

        engs3 = ['dve', 'act', 'dve']

        def load_cast(dst, src, rows, cols, stg, cnt):
            st = stg[cnt % 2]
            b.dma('sp', st[:rows, :cols], src)
            b.copy(engs3[cnt % 3], dst, st[:rows, :cols])

        iota128 = sb("iota128", [128, 128])
        b.op('pool', lambda: pool.iota(iota128[:], pattern=[[1, 128]], base=0, channel_multiplier=0,
                                       allow_small_or_imprecise_dtypes=True), [], [iota128])
        with ExitStack() as pp:
            pst = [sb("ppst%d" % i, [128, 2048], F32, pp) for i in range(3)]
            pob = [sb("ppob%d" % i, [128, 2048], BF16, pp) for i in range(3)]
            cnt = 0
            for l_ in range(nlayers):
                for k in range(8):
                    for c0 in range(0, NEXP, 2048):
                        st = pst[cnt % 3]; ob = pob[cnt % 3]
                        b.dma('sp', st[:], puT[l_, k * 128:(k + 1) * 128, c0:c0 + 2048], rkey='-')
                        b.copy('dve' if cnt % 2 == 0 else 'act', ob[:], st[:])
                        b.dma('pool', UT[l_, k * 128:(k + 1) * 128, c0:c0 + 2048], ob[:], wkey='-')
                        cnt += 1
                for r0_ in range(0, NEXP, 256):
                    st = pst[cnt % 3]; ob = pob[cnt % 3]
                    b.dma('sp', st[:].rearrange("p (j f) -> p j f", j=2),
                          pv[l_, r0_:r0_ + 256, :].rearrange("(j p) f -> p j f", p=128), rkey='-')
                    b.copy('dve' if cnt % 2 == 0 else 'act', ob[:], st[:])
                    b.dma('pool', VB[l_, r0_:r0_ + 256, :].rearrange("(j p) f -> p j f", p=128),
                          ob[:].rearrange("p (j f) -> p j f", j=2), wkey='-')
                    cnt += 1
            b.barrier()

        try:
            for l in range(nlayers):
                Hin = Hs[l % 2]; Hout = Hs[(l + 1) % 2]
                last = (l == DEPTH - 1)
                with ExitStack() as p1:
                    winb = sb("winb", [128, 8, INW], BF16, p1)
                    wpbb = sb("wpbb", [128, 8, D], BF16, p1)
                    wgk2 = sb("wgk2", [16, 512], F32, p1); bgk = sb("bgk", [1, 512], F32, p1)
                    gngb = sb("gngb", [128, 256], F32, p1)
                    with ExitStack() as p1s:
                        stg = [sb("stg%d" % i, [128, 2048], F32, p1s) for i in range(2)]
                        cnt = 0
                        for k in range(8):
                            for c0 in range(0, INW, 2048):
                                cw = min(2048, INW - c0)
                                load_cast(winb[:, k, c0:c0 + cw], w_in[l, k * 128:(k + 1) * 128, c0:c0 + cw], 128, cw, stg, cnt); cnt += 1
                            load_cast(wpbb[:, k, :], w_pb[l, k * 128:(k + 1) * 128, :], 128, D, stg, cnt); cnt += 1
                        b.barrier()
                    b.dma('sp', wgk2[:], w_gk2[l]); b.dma('sp', bgk[:], b_gk[l:l + 1, :])
                    b.dma('sp', gngb[:], gng[l].partition_broadcast(128))
                    if l == 0:
                        load_ln(ln_in_g[0], ln_in_b[0])
                    chk('p1w')

                    xt = [sb("xt%d" % i, [128, D], F32, p1) for i in range(2)]
                    hT_ = sb("hT", [128, 8, 128], BF16, p1)
                    glrT = sb("glrT", [16, 128], F32, p1)
                    ex = sb("ex", [128, 512], F32, p1); sp = sb("sp", [128, 512], F32, p1)
                    EbT = sb("EbT", [128, 512], F32, p1); EnbT = sb("EnbT", [128, 512], F32, p1); Erev = sb("Erev", [128, 512], F32, p1)
                    QeT = sb("QeT", [128, 4, 128], BF16, p1); KeT = sb("KeT", [128, 4, 128], BF16, p1)
                    Kd = sb("Kd", [128, 512], BF16, p1); vB = sb("vB", [128, D], BF16, p1)
                    Vb = sb("Vb", [128, 512], BF16, p1)
                    Vf = ex; Kf = sp
                    srb = sb("srb", [128, D], F32, p1); sgb = sb("sgb", [128, D], F32, p1)
                    osq = ln_xc; sga = ln_xc; og = ln_j; gby = ln_j
                    QTt = sb("QTt", [128, 4, 128], BF16, p1); KTt = sb("KTt", [128, 4, 128], BF16, p1)
                    scTm = sb("scTm", [128, 4, 128], BF16, p1)
                    Sst = {'p': sb("Sp", [128, 4, 256], F32, p1), 's': sb("Ss", [128, 4, 256], F32, p1)}
                    Sbf = {'p': sb("Spb", [128, 4, 256], BF16, p1), 's': sb("Ssb", [128, 4, 256], BF16, p1)}
                    ssq = sb("ssq", [128, 4], F32, p1)
                    ogb = sb("ogb", [128, D], BF16, p1)
                    ogT = sb("ogT", [128, 8, 128], BF16, p1)

                    b.memset('pool', Sst['p'][:], 0.0); b.memset('pool', Sbf['p'][:], 0.0)
                    b.dma('sp', Sst['s'][:], sg[l].rearrange("h k v -> k h v"))
                    b.copy('pool', Sbf['s'][:], Sst['s'][:])

                    def src_ap(ti):
                        n, t0, kind, r0 = tiles[ti]
                        if l > 0:
                            return Hin[t0:t0 + n, :]
                        return {'meta': meta, 'prompt': xp, 'sample': xs}[kind][r0:r0 + n, :]

                    b.dma('sp', xt[0][:tiles[0][0]], src_ap(0), rkey='-')
                    for ti, (n, t0, kind, r0) in enumerate(tiles):
                        x_ = xt[ti % 2]
                        cur_ti[0] = ti
                        if ti + 1 < len(tiles):
                            b.dma('sp', xt[(ti + 1) % 2][:tiles[ti + 1][0]], src_ap(ti + 1), rkey='-')
                        if l == 0:
                            h_ = srb
                            layernorm(h_[:n], x_[:n], n)
                            b.dma('pool', Hin[t0:t0 + n, :], h_[:n], wkey='-')
                            hsrc = h_
                        else:
                            hsrc = x_
                        transpose_to(hT_, hsrc[:n], n, False, (PS[0], PS[1]))
                        chk('p1t')
                        sk = 'p' if kind != 'sample' else 's'
                        S_ = Sst[sk]; Sb_ = Sbf[sk]

                        def fm_group(bank, col0, nchunk, width, boff=0):
                            for c in range(nchunk):
                                for k in range(8):
                                    b.mm(bank[:width, boff + c * n: boff + (c + 1) * n],
                                         winb[:, k, col0 + c * width: col0 + (c + 1) * width], hT_[:, k, :n],
                                         start=(k == 0), stop=(k == 7))

                        def tm_group(bank, col0, w=512):
                            for k in range(8):
                                b.mm(bank[:n, :w], hT_[:, k, :n], winb[:, k, col0:col0 + w], start=(k == 0), stop=(k == 7))

                        def v4(ap):
                            return ap.rearrange("p (c t) -> p c t", c=4)

                        fm_group(PS[2], C_GLR, 1, 16)
                        b.copy('dve', glrT[:, :n], PS[2][:16, :n])
                        b.mm(PS[3][:n, :], glrT[:16, :n], wgk2[:], start=True, stop=False)
                        b.mm(PS[3][:n, :], onesrow[0:1, :n], bgk[:], start=False, stop=True)
                        b.act(ex[:n], PS[3][:n, :], AF.Exp, scale=-1.0)
                        b.act(sp[:n], ex[:n], AF.Ln, bias=1.0)
                        for hh in range(4):
                            b.mm(PS[4][:, hh * n:(hh + 1) * n], sp[:n, hh * 128:(hh + 1) * 128], triInc[:n, :n])
                        b.mm(PS[5][:n, :], triGT[:n, :n], sp[:n, :])
                        b.act(EbT[:, :4 * n], PS[4][:, :4 * n], AF.Exp)
                        b.act(EnbT[:, :4 * n], PS[4][:, :4 * n], AF.Exp, scale=-1.0)
                        b.act(Erev[:n], PS[5][:n, :], AF.Exp)
                        fm_group(PS[6], C_QB, 4, 128)
                        b.stt('dve', QeT[:, :, :n], v4(PS[6][:, :4 * n]), 128.0 ** -0.5, v4(EbT[:, :4 * n]), ALU.mult, ALU.mult)
                        fm_group(PS[7], C_KB, 4, 128)
                        b.tt('dve', KeT[:, :, :n], v4(PS[7][:, :4 * n]), v4(EnbT[:, :4 * n]), ALU.mult)
                        tm_group(PS[2], C_KB)
                        b.tt('dve', Kd[:n], PS[2][:n, :], Erev[:n], ALU.mult)
                        tm_group(PS[3], C_VB); b.copy('act', vB[:n, 0:512], PS[3][:n, :])
                        tm_group(PS[4], C_VB + 512); b.copy('act', vB[:n, 512:1024], PS[4][:n, :])
                        chk('p1gate')
                        for hh in range(4):
                            b.mm(PS[5][:n, hh * n:(hh + 1) * n], KeT[:, hh, :n], QeT[:, hh, :n])
                        for hh in range(4):
                            b.tt('dve', scTm[:n, hh, :n], PS[5][:n, hh * n:(hh + 1) * n], maskLE[:n, :n], ALU.mult)
                        for hh in range(4):
                            ob = PS[6 + hh // 2][:n, (hh % 2) * 256:(hh % 2 + 1) * 256]
                            b.mm(ob, QeT[:, hh, :n], Sb_[:, hh, :], start=True, stop=False)
                            b.mm(ob, scTm[:n, hh, :n], vB[:n, hh * 256:(hh + 1) * 256], start=False, stop=True)
                        for hh in range(4):
                            db = PS[hh // 2][:, (hh % 2) * 256:(hh % 2 + 1) * 256]
                            b.mm(db, Kd[:n, hh * 128:(hh + 1) * 128], vB[:n, hh * 256:(hh + 1) * 256])
                        for hh in range(4):
                            db = PS[hh // 2][:, (hh % 2) * 256:(hh % 2 + 1) * 256]
                            b.stt('dve', S_[:, hh, :], S_[:, hh, :], EbT[:, hh * n + n - 1: hh * n + n], db, ALU.mult, ALU.add)
                        b.copy('act', Sb_[:], S_[:])
                        chk('p1gla')
                        for half in range(2):
                            b.act(osq[:n, half * 512:(half + 1) * 512], PS[6 + half][:n, :], AF.Square)
                        b.red('dve', ssq[:n], osq[:n].rearrange("p (h v) -> p h v", h=4), ALU.add)
                        b.ts('dve', ssq[:n], ssq[:n], 1.0 / 256.0, RMS_EPS, ALU.mult, ALU.add)
                        rsqrt_inplace(ssq[:n])
                        for hh in range(4):
                            b.stt('dve', og[:n, hh * 256:(hh + 1) * 256], PS[6 + hh // 2][:n, (hh % 2) * 256:(hh % 2 + 1) * 256],
                                  ssq[:n, hh:hh + 1], gngb[:n, :], ALU.mult, ALU.mult)
                        tm_group(PS[2], C_RB); b.act(srb[:n, 0:512], PS[2][:n, :], AF.Silu)
                        tm_group(PS[3], C_RB + 512); b.act(srb[:n, 512:1024], PS[3][:n, :], AF.Silu)
                        b.tt('dve', ogb[:n], og[:n], srb[:n], ALU.mult)
                        transpose_to(ogT, ogb[:n], n, True, (PS[4], PS[5]))
                        tm_group(PS[6], C_GB); b.act(sgb[:n, 0:512], PS[6][:n, :], AF.Sigmoid)
                        tm_group(PS[7], C_GB + 512); b.act(sgb[:n, 512:1024], PS[7][:n, :], AF.Sigmoid)
                        for half in range(2):
                            bank = PS[half]
                            for k in range(8):
                                b.mm(bank[:n, :], ogT[:, k, :n], wpbb[:, k, half * 512:(half + 1) * 512], start=(k == 0), stop=(k == 7))
                            b.tt('dve', gby[:n, half * 512:(half + 1) * 512], bank[:n, :], sgb[:n, half * 512:(half + 1) * 512], ALU.mult)
                        b.dma('pool', GBY[t0:t0 + n, :], gby[:n], wkey='-')
                        tm_group(PS[2], C_GA); b.act(sga[:n, 0:512], PS[2][:n, :], AF.Sigmoid)
                        tm_group(PS[3], C_GA + 512); b.act(sga[:n, 512:1024], PS[3][:n, :], AF.Sigmoid)
                        b.dma('pool', SGA[t0:t0 + n, :], sga[:n], wkey='-')
                        chk('p1merge')
                        tm_group(PS[4], C_VA)
                        b.copy('act', Vb[:n], PS[4][:n, :])
                        b.dma('pool', Vs[t0:t0 + n, :], Vb[:n], wkey='-')
                        orow = None
                        if kind == 'sample':
                            orow = (nks[l, :, :], nvs[l, :, :])
                        elif kind == 'prompt' and r0 >= S - KEEP:
                            o0 = r0 - (S - KEEP)
                            orow = (nkp[l, o0:o0 + n, :], nvp[l, o0:o0 + n, :])
                        if orow is not None:
                            b.copy('dve', Vf[:n], PS[4][:n, :])
                            b.dma('pool', orow[1], Vf[:n], wkey='-')
                            tm_group(PS[5], C_KA)
                            b.copy('dve', Kf[:n], PS[5][:n, :])
                            b.dma('pool', orow[0], Kf[:n], wkey='-')
                        fm_group(PS[6], C_QA, 4, 128)
                        b.act(QTt[:, :, :n], v4(PS[6][:, :4 * n]), AF.Copy, scale=0.125)
                        b.dma('pool', QT[:, :, t0:t0 + n], QTt[:, :, :n], wkey='-')
                        fm_group(PS[7], C_KA, 4, 128)
                        b.copy('dve', KTt[:, :, :n], v4(PS[7][:, :4 * n]))
                        b.dma('pool', KT[:, :, t0:t0 + n], KTt[:, :, :n], wkey='-')
                        if kind == 'prompt' and ti == NT:
                            b.dma('pool', stp[l].rearrange("h k v -> k h v"), Sst['p'][:], wkey='-')
                        chk('p1tile')
                        if kind == 'sample':
                            b.dma('pool', sts[l].rearrange("h k v -> k h v"), Sst['s'][:], wkey='-')
                    b.barrier()

                chk('p1')
                with ExitStack() as p2:
                    wpab = sb("wpab", [64, 8, D], BF16, p2); wob = sb("wob", [128, 8, D], BF16, p2)
                    Mm = [sb("Mm%d" % m, [128, 512], BF16, p2) for m in range(4)]
                    NMm = [sb("NMm%d" % m, [128, 512], BF16, p2) for m in range(4)]
                    QTb = [sb("QTb%d" % i, [128, 4, 512], BF16, p2) for i in range(2)]
                    eb = [sb("eb%d" % i, [128, 512], F32, p2) for i in range(3)]
                    spb = [sb("spb%d" % i, [128, 512], BF16, p2) for i in range(5)]
                    wb = [sb("wb%d" % i, [128, 512], BF16, p2) for i in range(4)]
                    Rb = [sb("Rb%d" % i, [128, 512], BF16, p2) for i in range(2)]
                    sbT = [sb("sbT%d" % i, [64, 8, 512], BF16, p2) for i in range(2)]
                    sgat = sb("sgat", [128, D], F32, p2); gbyt = sb("gbyt", [128, D], F32, p2); hres = sb("hres", [128, D], F32, p2)
                    mT = sb("mT", [128, 8, 128], BF16, p2)
                    with ExitStack() as p2s:
                        stg2 = [sb("stg2%d" % i, [128, 1024], F32, p2s) for i in range(2)]
                        cnt = 0
                        for hh in range(8):
                            load_cast(wpab[:, hh, :], w_pa[l, hh * 64:(hh + 1) * 64, :], 64, D, stg2, cnt); cnt += 1
                            load_cast(wob[:, hh, :], w_o[l, hh * 128:(hh + 1) * 128, :], 128, D, stg2, cnt); cnt += 1
                        for m in range(4):
                            st = stg2[m % 2]
                            b.memset('pool', st[:, :512], 1.0)
                            aff(st[:, :512], st[:, :512], [[1, 512]], ALU.is_gt, -128 * m, -1)
                            b.copy('dve', Mm[m][:], st[:, :512])
                            b.ts('dve', NMm[m][:], st[:, :512], 30000.0, -30000.0, ALU.mult, ALU.add)
                        b.barrier()
                    load_ln(ln1_g[l], ln1_b[l])

                    def run_attention(qblocks, prep):
                        tasks = []
                        for qi, (nq, q0, kbs) in enumerate(qblocks):
                            for hd in range(8):
                                for bi, kb in enumerate(kbs):
                                    tasks.append((qi, hd, bi, kb, bi == 0, bi == len(kbs) - 1))
                        state = {'q': -1}

                        def ensure_q(qi):
                            if state['q'] < qi:
                                nq, q0, _ = qblocks[qi]
                                b.dma('sp', QTb[qi % 2][:, :, :nq], QT[:, :, q0:q0 + nq], rkey='-')
                                state['q'] = qi

                        def qT_ap(qi, hd, nq):
                            po = (hd % 2) * 64
                            return QTb[qi % 2][po:po + 64, hd // 2, :nq]

                        def stage1(ix):
                            qi, hd, bi, kb, first, lastb = tasks[ix]
                            nq = qblocks[qi][0]; nk, m, kTf, vf = kb
                            ensure_q(qi)
                            if qi + 1 < len(qblocks) and hd == 4 and bi == 0:
                                ensure_q(qi + 1)
                            if prep is not None and bi == 0:
                                prep(qi, hd)
                            zA = PS[ix % 2]
                            b.mm(zA[:nk, :nq], kTf(hd), qT_ap(qi, hd, nq))
                            b.act(eb[ix % 3][:nk, :nq], zA[:nk, :nq], AF.Exp)

                        def stage1b(ix):
                            qi, hd, bi, kb, first, lastb = tasks[ix]
                            nq = qblocks[qi][0]; nk, m, kTf, vf = kb
                            b.act(spb[ix % 5][:nk, :nq], eb[ix % 3][:nk, :nq], AF.Ln, bias=1.0)
                            if m is not None:
                                b.tt('dve', spb[ix % 5][:nk, :nq], spb[ix % 5][:nk, :nq], Mm[m][:nk, :nq], ALU.mult)

                        def stage2(ix):
                            qi, hd, bi, kb, first, lastb = tasks[ix]
                            nq = qblocks[qi][0]; nk, m, kTf, vf = kb
                            R = Rb[(qi * 8 + hd) % 2]
                            Bk = PS[2 + ix % 2]
                            sp_ = spb[ix % 5]
                            if first:
                                b.memset('dve', R[:, :nq], 0.0)
                            b.mm(Bk[:nk, :nq], kTf(hd), qT_ap(qi, hd, nq), start=True, stop=False)
                            if not first:
                                b.mm(Bk[:nk, :nq], negOnes[:, :nk], R[:, :nq], start=False, stop=False)
                            if m is not None:
                                b.mm(Bk[:nk, :nq], identB[:nk, :nk], NMm[m][:nk, :nq], start=False, stop=False)
                            b.mm(Bk[:nk, :nq], negT[:nk, :nk], sp_[:nk, :nq], start=False, stop=True)
                            b.act(wb[ix % 4][:nk, :nq], Bk[:nk, :nq], AF.Exp)
                            if not lastb:
                                b.tt('dve', R[:nk, :nq], R[:nk, :nq], sp_[:nk, :nq], ALU.add)

                        def stage3(ix):
                            qi, hd, bi, kb, first, lastb = tasks[ix]
                            nq = qblocks[qi][0]; nk, m, kTf, vf = kb
                            O = PS[4 + (qi * 8 + hd) % 2]
                            b.mm(O[:64, :nq], vf(hd), wb[ix % 4][:nk, :nq], start=first, stop=lastb)
                            if lastb:
                                b.copy('dve', sbT[qi % 2][:, hd, :nq], O[:64, :nq])
                                if hd == 7:
                                    epilogue(qi)

                        def epilogue(qi):
                            nq, q0, _ = qblocks[qi]
                            for s0 in range(0, nq, 128):
                                n = min(128, nq - s0)
                                t0 = q0 + s0
                                b.dma('sp', sgat[:n], SGA[t0:t0 + n, :], rkey='-')
                                b.dma('sp', gbyt[:n], GBY[t0:t0 + n, :], rkey='-')
                                b.dma('sp', hres[:n], Hin[t0:t0 + n, :], rkey='-')
                                for half in range(2):
                                    bank = PS[6 + half]
                                    for hd in range(8):
                                        b.mm(bank[:n, :], sbT[qi % 2][:, hd, s0:s0 + n], wpab[:, hd, half * 512:(half + 1) * 512],
                                             start=(hd == 0), stop=(hd == 7))
                                    b.tt('dve', sgat[:n, half * 512:(half + 1) * 512], bank[:n, :], sgat[:n, half * 512:(half + 1) * 512], ALU.mult)
                                b.tt('dve', sgat[:n], sgat[:n], gbyt[:n], ALU.add)
                                transpose_to(mT, sgat[:n], n, False, (PS[6], PS[7]))
                                for half in range(2):
                                    bank = PS[6 + half]
                                    for k in range(8):
                                        b.mm(bank[:n, :], mT[:, k, :n], wob[:, k, half * 512:(half + 1) * 512], start=(k == 0), stop=(k == 7))
                                    b.stt('dve', hres[:n, half * 512:(half + 1) * 512], hres[:n, half * 512:(half + 1) * 512], ALPHA,
                                          bank[:n, :], ALU.mult, ALU.add)
                                layernorm(gbyt[:n], hres[:n], n)
                                b.dma('pool', X1[t0:t0 + n, :], gbyt[:n], wkey='-')

                        NTK = len(tasks)
                        for i in range(NTK + 4):
                            if i < NTK:
                                stage1(i)
                            if 0 <= i - 1 < NTK:
                                stage1b(i - 1)
                            if 0 <= i - 3 < NTK:
                                stage2(i - 3)
                            if 0 <= i - 4 < NTK:
                                stage3(i - 4)

                    with ExitStack() as p2a:
                        cks = sb("cks", [128, 4, P], BF16, p2a); cvs = sb("cvs", [128, P // 128, 512], BF16, p2a)
                        KTm = sb("KTm", [128, 4, 32], BF16, p2a); Vm = sb("Vm", [16, 2, 512], BF16, p2a)
                        with ExitStack() as p2as:
                            stg2 = [sb("stg2a%d" % i, [128, 2048], F32, p2as) for i in range(2)]
                            cnt = 0
                            for c in range(4):
                                for c0 in range(0, P, 2048):
                                    cw = min(2048, P - c0)
                                    load_cast(cks[:, c, c0:c0 + cw], ckT[l, :, c, c0:c0 + cw], 128, cw, stg2, cnt); cnt += 1
                            for j in range(0, P // 128, 4):
                                jn = min(4, P // 128 - j)
                                st = stg2[cnt % 2]
                                b.dma('sp', st[:, :jn * 512].rearrange("p (j f) -> p j f", j=jn),
                                      cvv[l, j * 128:(j + jn) * 128, :].rearrange("(j p) f -> p j f", p=128))
                                b.copy(engs3[cnt % 3], cvs[:, j:j + jn, :], st[:, :jn * 512].rearrange("p (j f) -> p j f", j=jn)); cnt += 1
                            b.barrier()
                        b.dma('sp', KTm[:, :, 0:16], KT[:, :, 0:16], rkey='-')
                        b.dma('sp', KTm[:, :, 16:32], KT[:, :, 16 + S:TOK], rkey='-')
                        b.dma('sp', Vm[:, 0, :], Vs[0:16, :], rkey='-')
                        b.dma('sp', Vm[:, 1, :], Vs[16 + S:TOK, :], rkey='-')

                        def kT_small(c0):
                            return lambda hd: KTm[(hd % 2) * 64:(hd % 2) * 64 + 64, hd // 2, c0:c0 + 16]

                        def v_small(t):
                            return lambda hd: Vm[:16, t, hd * 64:(hd + 1) * 64]

                        def kT_cache(j):
                            return lambda hd: cks[(hd % 2) * 64:(hd % 2) * 64 + 64, hd // 2, j * 128:(j + 1) * 128]

                        def v_cache(j):
                            return lambda hd: cvs[:, j, hd * 64:(hd + 1) * 64]

                        kb_meta = (16, None, kT_small(0), v_small(0))
                        qbA = [(16, 0, [(16, 0, kT_small(0), v_small(0))])]
                        kbs = [(16, 0, kT_small(16), v_small(1))]
                        for j in range(P // 128 - 1, -1, -1):
                            kbs.append((128, None, kT_cache(j), v_cache(j)))
                        kbs.append(kb_meta)
                        qbA.append((16, 16 + S, kbs))
                        run_attention(qbA, None)
                        b.barrier()

                    chk('p2a')
                    with ExitStack() as p2b:
                        KTp = [sb("KTp%d" % i, [128, S + 16], BF16, p2b) for i in range(2)]
                        Vp = [sb("Vp%d" % i, [128, NT + 1, 128], BF16, p2b) for i in range(2)]
                        pst = {'n': -1}

                        def load_pair(pc):
                            if pst['n'] >= pc or pc >= 4 * (S // 512):
                                return
                            qi, c = pc // 4, pc % 4
                            ntok = 16 + 512 * (qi + 1)
                            ntile = 4 * (qi + 1)
                            b.dma('sp', KTp[pc % 2][:, :ntok], KT[:, c, 0:ntok], rkey='-')
                            b.dma('sp', Vp[pc % 2][:16, 0, :], Vs[0:16, c * 128:(c + 1) * 128], rkey='-')
                            for j0 in range(0, ntile, 8):
                                jn = min(8, ntile - j0)
                                b.dma('sp', Vp[pc % 2][:, 1 + j0:1 + j0 + jn, :],
                                      Vs[16 + 128 * j0:16 + 128 * (j0 + jn), c * 128:(c + 1) * 128].rearrange("(j p) f -> p j f", p=128), rkey='-')
                            pst['n'] = pc

                        def prep(qi, hd):
                            pc = qi * 4 + hd // 2
                            if hd % 2 == 0:
                                load_pair(pc)
                            else:
                                load_pair(pc + 1)

                        def kT_p(qi, c0, nk):
                            return lambda hd: KTp[(qi * 4 + hd // 2) % 2][(hd % 2) * 64:(hd % 2) * 64 + 64, c0:c0 + nk]

                        def v_p(qi, vt, nk):
                            return lambda hd: Vp[(qi * 4 + hd // 2) % 2][:nk, vt, (hd % 2) * 64:(hd % 2) * 64 + 64]

                        qbB = []
                        for i in range(S // 512):
                            kbs = []
                            for m in range(3, -1, -1):
                                j = 4 * i + m
                                kbs.append((128, m, kT_p(i, 16 + 128 * j, 128), v_p(i, 1 + j, 128)))
                            for j in range(4 * i - 1, -1, -1):
                                kbs.append((128, None, kT_p(i, 16 + 128 * j, 128), v_p(i, 1 + j, 128)))
                            kbs.append((16, None, kT_p(i, 0, 16), v_p(i, 0, 16)))
                            qbB.append((512, 16 + 512 * i, kbs))
                        run_attention(qbB, prep)
                        b.barrier()
                    b.barrier()

                chk('p2')
                with ExitStack() as p3:
                    wqb = sb("wqb", [128, 8, 2048], BF16, p3)
                    skTs = sb("skTs", [128, 2, 128], F32, p3)
                    with ExitStack() as p3s:
                        stg3 = [sb("stg3%d" % i, [128, 2048], F32, p3s) for i in range(2)]
                        for k in range(8):
                            load_cast(wqb[:, k, :], wq[l, k * 128:(k + 1) * 128, :], 128, 2048, stg3, k)
                        b.barrier()
                    b.dma('sp', skTs[:], skT[l].rearrange("p d k -> d p k"))
                    load_ln(ln2_g[l], ln2_b[l])
                    x1 = [sb("x1_%d" % i, [128, D], F32, p3) for i in range(2)]
                    x1T = sb("x1T", [128, 8, 128], BF16, p3)
                    qT = sb("qT", [128, 16, 128], F32, p3)
                    Ssb = sb("Ssb", [128, 16, 128], F32, p3); S2 = sb("S2", [128, 16, 128], F32, p3)
                    sv = sb("sv", [128, 16, 16], F32, p3); si = sb("si", [128, 16, 16], U32, p3); sif = sb("sif", [128, 16, 16], F32, p3)
                    cand = sb("cand", [128, 8, 256], F32, p3); cand2 = sb("cand2", [128, 8, 256], F32, p3)
                    cv = sb("cv", [128, 8, 16], F32, p3); ci = sb("ci", [128, 8, 16], U32, p3)
                    ii = sb("ii", [128, 128], U32, p3); jj = sb("jj", [128, 128], U32, p3)
                    iif = sb("iif", [128, 128], F32, p3); jjf = sb("jjf", [128, 128], F32, p3)
                    eq = sb("eq", [128, 128, 16], F32, p3)
                    smx = sb("smx", [128, 8], F32, p3); sme = sb("sme", [128, 8, 16], F32, p3); smz = sb("smz", [128, 8], F32, p3)
                    selt = sb("selt", [128, 3, 128], F32, p3)
                    i1f = selt[:, 0, :]; i2f = selt[:, 1, :]; gg = selt[:, 2, :]

                    def top16(vals, vals2, ov, oi):
                        b.op('dve', lambda: vec.max(out=ov[:, 0:8], in_=vals), [vals], [ov])
                        b.op('dve', lambda: vec.max_index(out=oi[:, 0:8], in_max=ov[:, 0:8], in_values=vals), [ov, vals], [oi])
                        b.op('dve', lambda: vec.match_replace(out=vals2, in_to_replace=ov[:, 0:8], in_values=vals, imm_value=-1e30),
                             [ov, vals], [vals2])
                        b.op('dve', lambda: vec.max(out=ov[:, 8:16], in_=vals2), [vals2], [ov])
                        b.op('dve', lambda: vec.max_index(out=oi[:, 8:16], in_max=ov[:, 8:16], in_values=vals2), [ov, vals2], [oi])

                    b.dma('sp', x1[0][:tiles[0][0]], X1[0:tiles[0][0], :], rkey='-')
                    for ti, (n, t0, kind, r0) in enumerate(tiles):
                        x_ = x1[ti % 2]
                        if ti + 1 < len(tiles):
                            n1, t1 = tiles[ti + 1][0], tiles[ti + 1][1]
                            b.dma('sp', x1[(ti + 1) % 2][:n1], X1[t1:t1 + n1, :], rkey='-')
                        transpose_to(x1T, x_[:n], n, False, (PS[0], PS[1]))
                        for c in range(16):
                            bank = PS[2 + c // 4]
                            for k in range(8):
                                b.mm(bank[:, (c % 4) * n:(c % 4 + 1) * n], wqb[:, k, c * 128:(c + 1) * 128], x1T[:, k, :n],
                                     start=(k == 0), stop=(k == 7))
                            if c % 4 == 3:
                                b.copy('act' if (c // 4) % 2 else 'dve', qT[:, c - 3:c + 1, :n],
                                       bank[:, :4 * n].rearrange("p (c t) -> p c t", c=4))
                        for c in range(16):
                            bank = PS[6 + (c // 4) % 2]
                            b.mm(bank[:n, (c % 4) * 128:(c % 4 + 1) * 128], qT[:, c, :n], skTs[:, c % 2, :])
                            if c % 4 == 3:
                                b.copy('act' if (c // 4) % 2 else 'dve', Ssb[:n, c - 3:c + 1, :],
                                       bank[:n, :].rearrange("p (c k) -> p c k", c=4))
                        for c in range(16):
                            top16(Ssb[:n, c, :], S2[:n, c, :], sv[:n, c, :], si[:n, c, :])
                        b.copy('dve', sif[:n], si[:n])
                        for hd in range(8):
                            b.tt('dve', cand[:n, hd, :].rearrange("p (i j) -> p i j", i=16),
                                 sv[:n, 2 * hd, :].unsqueeze(2).to_broadcast([n, 16, 16]),
                                 sv[:n, 2 * hd + 1, :].unsqueeze(1).to_broadcast([n, 16, 16]), ALU.add)
                        for hd in range(8):
                            top16(cand[:n, hd, :], cand2[:n, hd, :], cv[:n, hd, :], ci[:n, hd, :])
                        cif = ci[:n].rearrange("p h k -> p (h k)")
                        b.op('dve', lambda: vec.tensor_single_scalar(out=ii[:n], in_=cif, scalar=4, op=ALU.logical_shift_right), [ci], [ii])
                        b.op('dve', lambda: vec.tensor_single_scalar(out=jj[:n], in_=cif, scalar=15, op=ALU.bitwise_and), [ci], [jj])
                        b.copy('dve', iif[:n], ii[:n]); b.copy('dve', jjf[:n], jj[:n])
                        sif4 = sif[:n].rearrange("p (h two) k -> p h two k", two=2)
                        for (srcf, half, dst) in ((iif, 0, i1f), (jjf, 1, i2f)):
                            b.tt('dve', eq[:n], srcf[:n].unsqueeze(2).to_broadcast([n, 128, 16]),
                                 iota16[:n].unsqueeze(1).to_broadcast([n, 128, 16]), ALU.is_equal)
                            eq4 = eq[:n].rearrange("p (h c) i -> p h c i", h=8)
                            b.tt('dve', eq4, eq4, sif4[:, :, half, :].unsqueeze(2).to_broadcast([n, 8, 16, 16]), ALU.mult)
                            b.red('dve', dst[:n], eq[:n], ALU.add)
                        b.red('dve', smx[:n], cv[:n], ALU.max)
                        b.tt('dve', sme[:n], cv[:n], smx[:n].unsqueeze(2).to_broadcast([n, 8, 16]), ALU.subtract)
                        b.act(sme[:n], sme[:n], AF.Exp)
                        b.red('dve', smz[:n], sme[:n], ALU.add)
                        b.op('dve', lambda: vec.reciprocal(out=smz[:n], in_=smz[:n]), [smz], [smz])
                        b.tt('dve', gg[:n].rearrange("p (h k) -> p h k", h=8), sme[:n], smz[:n].unsqueeze(2).to_broadcast([n, 8, 16]), ALU.mult)
                        chk('p3top')
                        b.dma('pool', SEL[t0:t0 + n, :, :], selt[:n], wkey='-')
                    b.barrier()

                chk('p3a')
                with ExitStack() as p3b:
                    Gall = sb("Gall", [128, 256, 128], BF16, p3b)
                    uTs = [sb("uTs%d" % i, [128, 8, 1024], BF16, p3b) for i in range(2)]
                    vSs = [sb("vSs%d" % i, [128, 8, 1024], BF16, p3b) for i in range(2)]
                    xg = [sb("xg%d" % i, [128, D], F32, p3b) for i in range(2)]
                    xgT = sb("xgT", [128, 8, 256], BF16, p3b)
                    selg = sb("selg", [128, 3, 128], F32, p3b); selT = sb("selT", [128, 3, 256], F32, p3b)
                    TB = 16
                    Call = [sb("Call%d" % i, [128, TB, 128], BF16, p3b) for i in range(2)]
                    Wall = [sb("Wall%d" % i, [128, TB, 128], BF16, p3b) for i in range(2)]
                    selTb = sb("selTb", [128, 3, 256], BF16, p3b)
                    iotab = sb("iotab", [128, 128], BF16, p3b)
                    b.copy('dve', iotab[:], iota128[:])
                    gel = [sb("gel%d" % i, [128, 256], F32, p3b) for i in range(2)]
                    AT = [sb("AT%d" % i, [128, 256], BF16, p3b) for i in range(3)]
                    acc = sb("acc", [128, D], F32, p3b); x2 = sb("x2", [128, D], F32, p3b)
                    groups = [[tiles[0]]] + [tiles[1 + 2 * g: 3 + 2 * g] for g in range(NT // 2)] + [[tiles[-1]]]
                    NSC = NEXP // 1024
                    wst = {'n': -1}

                    def load_w(gsc):
                        if wst['n'] >= gsc or gsc >= len(groups) * NSC:
                            return
                        sc = gsc % NSC
                        b.dma('sp', uTs[gsc % 2][:], UT[l, :, sc * 1024:(sc + 1) * 1024].rearrange("(k p) e -> p k e", p=128), rkey='-')
                        b.dma('sp', vSs[gsc % 2][:], VB[l, sc * 1024:(sc + 1) * 1024, :].rearrange("(c p) d -> p c d", p=128), rkey='-')
                        wst['n'] = gsc

                    for gi_, grp in enumerate(groups):
                        load_w(gi_ * NSC)
                        T = sum(t[0] for t in grp)
                        cs = 0
                        offs = []
                        for s_, (n, t0, kind, r0) in enumerate(grp):
                            offs.append(cs)
                            b.dma('sp', xg[s_][:n], X1[t0:t0 + n, :], rkey='-')
                            b.dma('sp', selg[:n], SEL[t0:t0 + n, :, :], rkey='-')
                            transpose_to(xgT[:, :, cs:cs + n], xg[s_][:n], n, False, (PS[4], PS[5]))
                            for w_ in range(3):
                                b.mm(PS[6][:, w_ * n:(w_ + 1) * n], selg[:n, w_, :], identF[:n, :n])
                            b.copy('dve', selT[:, :, cs:cs + n], PS[6][:, 0:3 * n].rearrange("p (w t) -> p w t", w=3))
                            cs += n
                        b.copy('act', selTb[:, :, :T], selT[:, :, :T])
                        for tk in range(T):
                            if tk % TB == 0:
                                bi_ = (tk // TB) % 2
                                iob = iotab[:].unsqueeze(1).to_broadcast([128, TB, 128])
                                b.tt('dve', Call[bi_][:], iob, selTb[:, 1, tk:tk + TB].unsqueeze(2).to_broadcast([128, TB, 128]), ALU.is_equal)
                                b.tt('dve', Wall[bi_][:], iob, selTb[:, 0, tk:tk + TB].unsqueeze(2).to_broadcast([128, TB, 128]), ALU.is_equal)
                                b.tt('dve', Wall[bi_][:], Wall[bi_][:], selTb[:, 2, tk:tk + TB].unsqueeze(2).to_broadcast([128, TB, 128]), ALU.mult)
                            c_ = Call[(tk // TB) % 2][:, tk % TB, :]; w_ = Wall[(tk // TB) % 2][:, tk % TB, :]
                            gp = PS[6 + (tk // 4) % 2]
                            b.mm(gp[:, (tk % 4) * 128:(tk % 4 + 1) * 128], c_, w_)
                            if tk % 4 == 3:
                                b.copy('act', Gall[:, tk - 3:tk + 1, :], gp[:, :].rearrange("p (t i) -> p t i", t=4))

                        def Hmm(ch):
                            gsc = gi_ * NSC + ch // 8
                            for k in range(8):
                                b.mm(PS[4 + ch % 2][:, :T], uTs[gsc % 2][:, k, (ch % 8) * 128:(ch % 8 + 1) * 128], xgT[:, k, :T],
                                     start=(k == 0), stop=(k == 7))

                        Hmm(0)
                        for ch in range(128):
                            gsc = gi_ * NSC + ch // 8
                            if ch % 8 == 0:
                                load_w(gsc + 1)
                            if ch + 1 < 128:
                                Hmm(ch + 1)
                            b.act(gel[ch % 2][:, :T], PS[4 + ch % 2][:, :T], AF.Gelu)
                            b.tt('dve', AT[ch % 3][:, :T], gel[ch % 2][:, :T], Gall[:, :T, ch], ALU.mult)
                            for s_, (n, t0, kind, r0) in enumerate(grp):
                                for half in range(2):
                                    b.mm(PS[2 * s_ + half][:n, :], AT[ch % 3][:, offs[s_]:offs[s_] + n],
                                         vSs[gsc % 2][:, ch % 8, half * 512:(half + 1) * 512], start=(ch == 0), stop=(ch == 127))
                        for s_, (n, t0, kind, r0) in enumerate(grp):
                            for half in range(2):
                                b.stt('dve', acc[:n, half * 512:(half + 1) * 512], xg[s_][:n, half * 512:(half + 1) * 512], ALPHA,
                                      PS[2 * s_ + half][:n, :], ALU.mult, ALU.add)
                            layernorm(x2[:n], acc[:n], n)
                            if not last:
                                b.dma('pool', Hout[t0:t0 + n, :], x2[:n], wkey='-')
                            elif kind == 'prompt':
                                b.dma('pool', yp[r0:r0 + n, :], x2[:n], wkey='-')
                            elif kind == 'sample':
                                b.dma('pool', ys[:, :], x2[:n], wkey='-')
                    b.barrier()
        except _Stop:
            pass
        b.stopped = False
        b.barrier()
    return nc


def make_in_maps(inputs, S, P, ncores):
    f = lambda a: np.ascontiguousarray(np.asarray(a, dtype=np.float32))
    ck = np.asarray(inputs['cache_k'], dtype=np.float32)
    cvv = np.asarray(inputs['cache_v'], dtype=np.float32)
    common = {
        "meta": f(inputs['meta']), "ln_in_g": f(inputs['ln_in_g']).reshape(1, D), "ln_in_b": f(inputs['ln_in_b']).reshape(1, D),
        "w_in": f(inputs['w_in']), "w_gk2": f(inputs['w_gk2']), "b_gk": f(inputs['b_gk']), "gng": f(inputs['gla_norm_g']),
        "w_pa": f(inputs['w_pa']), "w_pb": f(inputs['w_pb']), "w_o": f(inputs['w_o']),
        "ln1_g": f(inputs['ln1_g']), "ln1_b": f(inputs['ln1_b']), "wq": f(inputs['peer_wq']),
        "skT": f(np.asarray(inputs['peer_subkeys']).transpose(0, 1, 3, 2)),
        "puT": f(np.asarray(inputs['peer_u']).transpose(0, 2, 1)), "pv": f(inputs['peer_v']), "ln2_g": f(inputs['ln2_g']), "ln2_b": f(inputs['ln2_b']),
    }
    maps = []
    for c in range(ncores):
        m = dict(common)
        m["xp"] = f(inputs['x_prompt'][c]); m["xs"] = f(inputs['x_sample'][c])
        m["ckT"] = f(ck[:, c].reshape(DEPTH, P, 4, 2, 64).transpose(0, 3, 4, 2, 1).reshape(DEPTH, 128, 4, P))
        m["cv"] = f(cvv[:, c].reshape(DEPTH, P, 512))
        m["sg"] = f(np.asarray(inputs['state_gla'])[:, c])
        maps.append(m)
    return maps


def assemble(results, S, P):
    KEEP = min(S, P)
    nb = len(results)
    yp = np.stack([r["yp"] for r in results]); ys = np.stack([r["ys"] for r in results])
    nkp = np.stack([r["nkp"] for r in results], axis=1).reshape(DEPTH, nb, KEEP, 8, 64)
    nvp = np.stack([r["nvp"] for r in results], axis=1).reshape(DEPTH, nb, KEEP, 8, 64)
    stp = np.stack([r["stp"] for r in results], axis=1)
    nks = np.stack([r["nks"] for r in results], axis=1).reshape(DEPTH, nb, TS, 8, 64)
    nvs = np.stack([r["nvs"] for r in results], axis=1).reshape(DEPTH, nb, TS, 8, 64)
    sts = np.stack([r["sts"] for r in results], axis=1)
    return tuple(np.ascontiguousarray(a, dtype=np.float32) for a in (yp, ys, nkp, nvp, stp, nks, nvs, sts))


def kernel(**inputs):
    S = inputs['x_prompt'].shape[1]; P = inputs['cache_k'].shape[2]
    nb = inputs['x_prompt'].shape[0]
    nc = build(S, P)
    maps = make_in_maps(inputs, S, P, nb)
    res = run_bass_kernel_spmd(nc, maps, core_ids=list(range(nb)))
    return assemble(res.results, S, P)
```
